# Optimizing a Trainium2 kernel written in Bass

```python
import jax, jax.numpy as jnp
from jax import lax
import numpy as np

D_MODEL = 1024
BATCH = 32
SEQ = 256
DEPTH = 4
DEC_BATCH = 2
DEC_SEQ = 1024
PAST_LEN = 512

GRID_W = 64
SSD_D_INNER = 1024
SSD_HEAD_DIM = 64
SSD_HEADS = SSD_D_INNER // SSD_HEAD_DIM
SSD_GROUPS = 4
SSD_D_STATE = 128
SSD_CONV = 5
SSD_CHUNK = 128
SSD_XBC = SSD_D_INNER + 2 * SSD_GROUPS * SSD_D_STATE
CONF_D = 512
CONF_KERNEL = 31
POOL_D = 512
POOL_WINDOWS = (2, 4, 8, 16)
POOL_GROUPS = len(POOL_WINDOWS)
POOL_GROUP_D = POOL_D // POOL_GROUPS
N_BRANCH = 3
D_FF = 2816
N_MOD = 9
FFN_RES = 0.5
EPS = 1e-6
OFF_Z = SSD_D_INNER
OFF_XBC = OFF_Z + SSD_XBC
OFF_DT = OFF_XBC + 2 * SSD_HEADS
OFF_CONF = OFF_DT + 2 * CONF_D
OFF_POOL = OFF_CONF + POOL_D
IN_COLS = OFF_POOL + N_BRANCH * D_MODEL

kernel_name = 'hybrid_ssd_conformer_pool_diffusion_step'


def rmsnorm(x, g):
    x32 = x.astype(jnp.float32)
    r = x32 * lax.rsqrt(jnp.mean(x32 * x32, axis=-1, keepdims=True) + EPS)
    return (r * g.astype(jnp.float32)).astype(x.dtype)


def layernorm(x, g, b):
    x32 = x.astype(jnp.float32)
    mu = jnp.mean(x32, axis=-1, keepdims=True)
    var = jnp.mean(jnp.square(x32 - mu), axis=-1, keepdims=True)
    r = (x32 - mu) * lax.rsqrt(var + EPS)
    return (r * g.astype(jnp.float32) + b.astype(jnp.float32)).astype(x.dtype)


def dwconv(x, w, b):
    k = w.shape[0]
    y = lax.conv_general_dilated(x, w[:, None, :].astype(x.dtype), window_strides=(1,),
                                 padding=[(k // 2, k // 2)],
                                 dimension_numbers=('NWC', 'WIO', 'NWC'),
                                 feature_group_count=x.shape[-1])
    return y + b.astype(x.dtype)


def swiglu(h, w_in, w_out):
    g, u = jnp.split(h @ w_in, 2, axis=-1)
    return (jax.nn.silu(g) * u) @ w_out


def pos_embed_2d(n_tokens):
    rows = n_tokens // GRID_W
    r, col = jnp.meshgrid(jnp.arange(rows), jnp.arange(GRID_W), indexing='ij')
    r = r.reshape(-1).astype(jnp.float32)
    col = col.reshape(-1).astype(jnp.float32)
    q = D_MODEL // 4
    omega = 1.0 / (10000.0 ** (jnp.arange(q, dtype=jnp.float32) / q))
    ar = r[:, None] * omega
    ac = col[:, None] * omega
    return jnp.concatenate([jnp.sin(ar), jnp.cos(ar), jnp.sin(ac), jnp.cos(ac)], axis=-1)


def ssd_scan(x, dt, a_neg, B, C, h0):
    b, L, H, P = x.shape
    N = B.shape[-1]
    nc = L // SSD_CHUNK
    xr = (x * dt[..., None]).reshape(b, nc, SSD_CHUNK, H, P)
    Br = B.reshape(b, nc, SSD_CHUNK, H, N)
    Cr = C.reshape(b, nc, SSD_CHUNK, H, N)
    a_cs = jnp.cumsum((dt * a_neg).reshape(b, nc, SSD_CHUNK, H), axis=2)
    tri = jnp.tril(jnp.ones((SSD_CHUNK, SSD_CHUNK), dtype=bool))[None, None, :, :, None]
    seg = a_cs[:, :, :, None, :] - a_cs[:, :, None, :, :]
    decay_ls = jnp.exp(jnp.where(tri, seg, -jnp.inf))
    scores = jnp.einsum('bclhn,bcshn->bclsh', Cr, Br) * decay_ls
    y_diag = jnp.einsum('bclsh,bcshp->bclhp', scores, xr)
    decay_to_end = jnp.exp(a_cs[:, :, -1:, :] - a_cs)
    chunk_states = jnp.einsum('bclhn,bclhp->bchpn', Br * decay_to_end[..., None], xr)
    states = jnp.concatenate([h0[:, None], chunk_states], axis=1)
    cs_pad = jnp.concatenate([jnp.zeros((b, 1, H), a_cs.dtype),
                              jnp.cumsum(a_cs[:, :, -1, :], axis=1)], axis=1)
    tri_c = jnp.tril(jnp.ones((nc + 1, nc + 1), dtype=bool))[None, :, :, None]
    seg_c = cs_pad[:, :, None, :] - cs_pad[:, None, :, :]
    decay_c = jnp.exp(jnp.where(tri_c, seg_c, -jnp.inf))
    new_states = jnp.einsum('bzch,bchpn->bzhpn', decay_c, states)
    y_off = jnp.einsum('bclhn,bchpn->bclhp', Cr * jnp.exp(a_cs)[..., None], new_states[:, :-1])
    return (y_diag + y_off).reshape(b, L, H, P), new_states[:, -1]


def ssd_branch(z, xbc, dt_raw, h0, ssd_conv_w, ssd_conv_b, ssd_a_log, ssd_dt_bias, ssd_d,
               ssd_norm_g, w_br_ssd):
    b, L, _ = xbc.shape
    xbc = jax.nn.silu(dwconv(xbc, ssd_conv_w, ssd_conv_b))
    xs, Bm, Cm = jnp.split(xbc, [SSD_D_INNER, SSD_D_INNER + SSD_GROUPS * SSD_D_STATE], axis=-1)
    rep = SSD_HEADS // SSD_GROUPS
    xh = xs.reshape(b, L, SSD_HEADS, SSD_HEAD_DIM).astype(jnp.float32)
    Bh = jnp.repeat(Bm.reshape(b, L, SSD_GROUPS, SSD_D_STATE), rep, axis=2).astype(jnp.float32)
    Ch = jnp.repeat(Cm.reshape(b, L, SSD_GROUPS, SSD_D_STATE), rep, axis=2).astype(jnp.float32)
    dt = jax.nn.softplus(dt_raw.astype(jnp.float32).reshape(b, L, 2, SSD_HEADS)
                         + ssd_dt_bias.astype(jnp.float32))
    a_neg = -jnp.exp(ssd_a_log.astype(jnp.float32))
    h0 = h0.astype(jnp.float32)
    y_f, s_f = ssd_scan(xh, dt[:, :, 0], a_neg[0], Bh, Ch, h0[:, 0])
    y_b, s_b = ssd_scan(jnp.flip(xh, 1), jnp.flip(dt[:, :, 1], 1), a_neg[1],
                        jnp.flip(Bh, 1), jnp.flip(Ch, 1), h0[:, 1])
    y = y_f + jnp.flip(y_b, 1) + ssd_d.astype(jnp.float32)[:, None] * xh
    y = y.reshape(b, L, SSD_D_INNER) * jax.nn.silu(z.astype(jnp.float32))
    y = rmsnorm(y, ssd_norm_g).astype(z.dtype)
    return y @ w_br_ssd, jnp.stack([s_f, s_b], axis=1)


def conformer_branch(u, conf_conv_w, conf_conv_b, conf_ln_g, conf_ln_b, w_br_conf):
    a, g = jnp.split(u, 2, axis=-1)
    v = a * jax.nn.sigmoid(g)
    v = dwconv(v, conf_conv_w, conf_conv_b)
    v = jax.nn.silu(layernorm(v, conf_ln_g, conf_ln_b))
    return v @ w_br_conf


def pool_branch(u, pool_w, pool_scale, w_br_pool):
    b, L, _ = u.shape
    u32 = u.astype(jnp.float32)
    cs = jnp.concatenate([jnp.zeros((b, 1, POOL_D), jnp.float32), jnp.cumsum(u32, axis=1)], axis=1)
    t = jnp.arange(L)
    groups = []
    for gi, w in enumerate(POOL_WINDOWS):
        lo = jnp.clip(t - w // 2, 0, L)
        hi = jnp.clip(t - w // 2 + w, 0, L)
        sl = slice(gi * POOL_GROUP_D, (gi + 1) * POOL_GROUP_D)
        cs_g = cs[..., sl]
        mean = (jnp.take(cs_g, hi, axis=1) - jnp.take(cs_g, lo, axis=1)) \
            / (hi - lo).astype(jnp.float32)[None, :, None]
        groups.append(mean - u32[..., sl])
    pooled = jnp.stack(groups, axis=2)
    mixed = jnp.einsum('blgc,gcd->blgd', pooled, pool_w.astype(jnp.float32)).reshape(b, L, POOL_D)
    mixed = mixed * pool_scale.astype(jnp.float32)
    return mixed.astype(u.dtype) @ w_br_pool


def token_mixer(h, h0, w_in, ssd_conv_w, ssd_conv_b, ssd_a_log, ssd_dt_bias, ssd_d, ssd_norm_g,
                w_br_ssd, conf_conv_w, conf_conv_b, conf_ln_g, conf_ln_b, w_br_conf, pool_w,
                pool_scale, w_br_pool, w_out):
    proj = h @ w_in
    z, xbc, dt_raw, conf_in, pool_in, gate_in = jnp.split(
        proj, [OFF_Z, OFF_XBC, OFF_DT, OFF_CONF, OFF_POOL], axis=-1)
    br_ssd, st = ssd_branch(z, xbc, dt_raw, h0, ssd_conv_w, ssd_conv_b, ssd_a_log, ssd_dt_bias,
                            ssd_d, ssd_norm_g, w_br_ssd)
    br_conf = conformer_branch(conf_in, conf_conv_w, conf_conv_b, conf_ln_g, conf_ln_b, w_br_conf)
    br_pool = pool_branch(pool_in, pool_w, pool_scale, w_br_pool)
    g_ssd, g_conf, g_pool = jnp.split(jax.nn.sigmoid(gate_in), N_BRANCH, axis=-1)
    merged = g_ssd * br_ssd + g_conf * br_conf + g_pool * br_pool
    return merged @ w_out, st


def layer(x, cond, h0, w_mod, b_mod, norm_g, w_ffn_in, w_ffn_out, w_in, ssd_conv_w, ssd_conv_b,
          ssd_a_log, ssd_dt_bias, ssd_d, ssd_norm_g, w_br_ssd, conf_conv_w, conf_conv_b, conf_ln_g,
          conf_ln_b, w_br_conf, pool_w, pool_scale, w_br_pool, w_out):
    mod = jax.nn.silu(cond) @ w_mod + b_mod
    sh1, sc1, g1, sh2, sc2, g2, sh3, sc3, g3 = jnp.split(mod[:, None, :], N_MOD, axis=-1)
    h = rmsnorm(x, norm_g[0]) * (1 + sc1) + sh1
    x = x + FFN_RES * g1 * rmsnorm(swiglu(h, w_ffn_in[0], w_ffn_out[0]), norm_g[1])
    h = rmsnorm(x, norm_g[2]) * (1 + sc2) + sh2
    y, st = token_mixer(h, h0, w_in, ssd_conv_w, ssd_conv_b, ssd_a_log, ssd_dt_bias, ssd_d,
                        ssd_norm_g, w_br_ssd, conf_conv_w, conf_conv_b, conf_ln_g, conf_ln_b,
                        w_br_conf, pool_w, pool_scale, w_br_pool, w_out)
    x = x + g2 * rmsnorm(y, norm_g[3])
    h = rmsnorm(x, norm_g[4]) * (1 + sc3) + sh3
    x = x + FFN_RES * g3 * rmsnorm(swiglu(h, w_ffn_in[1], w_ffn_out[1]), norm_g[5])
    return x, st


def setup_inputs(seed: int = 0) -> dict:
    key = jax.random.key(seed)
    ks = jax.random.split(key, 32)
    f32 = jnp.float32

    def nrm(k, shape, fan_in):
        return jax.random.normal(k, shape, f32) * (fan_in ** -0.5)

    def gain(k, shape):
        return 1.0 + 0.05 * jax.random.normal(k, shape, f32)

    def small(k, shape):
        return 0.02 * jax.random.normal(k, shape, f32)

    dt0 = jnp.exp(jax.random.uniform(ks[10], (DEPTH, 2, SSD_HEADS), f32,
                                     float(np.log(1e-3)), float(np.log(1e-1))))
    return {
        'x_prompt': jax.random.normal(ks[0], (BATCH, SEQ, D_MODEL), f32),
        'x_sample': jax.random.normal(ks[1], (DEC_BATCH, DEC_SEQ, D_MODEL), f32),
        'state_ssd': 0.5 * jax.random.normal(
            ks[2], (DEC_BATCH, DEPTH, 2, SSD_HEADS, SSD_HEAD_DIM, SSD_D_STATE), f32),
        'c': jax.random.normal(ks[3], (DEC_BATCH, D_MODEL), f32),
        'c_ctx': jax.random.normal(ks[4], (D_MODEL,), f32),
        'w_mod': nrm(ks[5], (DEPTH, D_MODEL, N_MOD * D_MODEL), D_MODEL),
        'b_mod': small(ks[6], (DEPTH, N_MOD * D_MODEL)),
        'norm_g': gain(ks[7], (DEPTH, 6, D_MODEL)),
        'w_ffn_in': nrm(ks[8], (DEPTH, 2, D_MODEL, 2 * D_FF), D_MODEL),
        'w_ffn_out': nrm(ks[9], (DEPTH, 2, D_FF, D_MODEL), D_FF),
        'w_in': nrm(ks[11], (DEPTH, D_MODEL, IN_COLS), D_MODEL),
        'ssd_conv_w': nrm(ks[12], (DEPTH, SSD_CONV, SSD_XBC), SSD_CONV),
        'ssd_conv_b': small(ks[13], (DEPTH, SSD_XBC)),
        'ssd_a_log': jnp.log(jax.random.uniform(ks[14], (DEPTH, 2, SSD_HEADS), f32, 1.0, 16.0)),
        'ssd_dt_bias': dt0 + jnp.log(-jnp.expm1(-dt0)),
        'ssd_d': 1.0 + 0.1 * jax.random.normal(ks[15], (DEPTH, SSD_HEADS), f32),
        'ssd_norm_g': gain(ks[16], (DEPTH, SSD_D_INNER)),
        'w_br_ssd': nrm(ks[17], (DEPTH, SSD_D_INNER, D_MODEL), SSD_D_INNER),
        'conf_conv_w': nrm(ks[18], (DEPTH, CONF_KERNEL, CONF_D), CONF_KERNEL),
        'conf_conv_b': small(ks[19], (DEPTH, CONF_D)),
        'conf_ln_g': gain(ks[20], (DEPTH, CONF_D)),
        'conf_ln_b': small(ks[21], (DEPTH, CONF_D)),
        'w_br_conf': nrm(ks[22], (DEPTH, CONF_D, D_MODEL), CONF_D),
        'pool_w': nrm(ks[23], (DEPTH, POOL_GROUPS, POOL_GROUP_D, POOL_GROUP_D), POOL_GROUP_D),
        'pool_scale': 1.0 + 0.1 * jax.random.normal(ks[24], (DEPTH, POOL_D), f32),
        'w_br_pool': nrm(ks[25], (DEPTH, POOL_D, D_MODEL), POOL_D),
        'w_out': nrm(ks[26], (DEPTH, D_MODEL, D_MODEL), D_MODEL),
    }


def reference(x_prompt, x_sample, state_ssd, c, c_ctx, w_mod, b_mod, norm_g, w_ffn_in, w_ffn_out,
              w_in, ssd_conv_w, ssd_conv_b, ssd_a_log, ssd_dt_bias, ssd_d, ssd_norm_g, w_br_ssd,
              conf_conv_w, conf_conv_b, conf_ln_g, conf_ln_b, w_br_conf, pool_w, pool_scale,
              w_br_pool, w_out):
    ctx_cond = c_ctx[None, :]
    xp = x_prompt
    zero_state = jnp.zeros((x_prompt.shape[0], 2, SSD_HEADS, SSD_HEAD_DIM, SSD_D_STATE), jnp.float32)
    xs = x_sample + pos_embed_2d(x_sample.shape[1]).astype(x_sample.dtype)[None]
    ctx_states = []
    for l in range(DEPTH):
        lp = (w_mod[l], b_mod[l], norm_g[l], w_ffn_in[l], w_ffn_out[l], w_in[l], ssd_conv_w[l],
              ssd_conv_b[l], ssd_a_log[l], ssd_dt_bias[l], ssd_d[l], ssd_norm_g[l], w_br_ssd[l],
              conf_conv_w[l], conf_conv_b[l], conf_ln_g[l], conf_ln_b[l], w_br_conf[l], pool_w[l],
              pool_scale[l], w_br_pool[l], w_out[l])
        xp, st = layer(xp, ctx_cond, zero_state, *lp)
        ctx_states.append(st)
        xs, _ = layer(xs, c, state_ssd[:, l], *lp)
    new_state_ssd = jnp.stack(ctx_states, axis=1).astype(x_prompt.dtype)
    return (xp, xs, new_state_ssd)
```

```python
import numpy as np
from contextlib import ExitStack
import concourse.bass as bass
import concourse.mybir as mybir
from concourse.bass_utils import run_bass_kernel_spmd

F32 = mybir.dt.float32
BF16 = mybir.dt.bfloat16
ALU = mybir.AluOpType
AF = mybir.ActivationFunctionType

D = 1024
DEPTH = 4
T = 1024
NT = 2
NCH = 8
DFF = 2816
EPS = 1e-6
C_Z, C_XBC, C_DT, C_CONF, C_POOL, C_GATE = 0, 1024, 3072, 3104, 4128, 4640
SEM_LIMIT = 8000
NCORES = 8

PP_NG, PP_BMOD, PP_SCW, PP_SCB, PP_SNG = 0, 48, 120, 200, 216
PP_CCW, PP_CCB, PP_CLG, PP_CLB, PP_PSC = 224, 348, 352, 356, 360
PP_ALOG, PP_DTB, PP_DD = 364, 396, 428
PPL = 444
CC_U, CC_LO, CC_SU, CC_SL, CC_ID, CC_N = 0, 128, 256, 384, 512, 640


class _Eng:
    def __init__(self, kb, name, handle):
        self.kb = kb
        self.name = name
        self.h = handle
        self.nsem = 0
        self.sem = kb.new_sem(f"{name}_s0")
        self.cnt = 0
        self.seen = {}
        self.in_chain = False

    def rotate(self):
        if self.cnt >= SEM_LIMIT:
            self.nsem += 1
            self.sem = self.kb.new_sem(f"{self.name}_s{self.nsem}")
            self.cnt = 0


class KB:
    def __init__(self, nc, stack):
        self.nc = nc
        self.stack = stack
        self.bufs = {}
        self.dsem = {}
        self.eng = {
            "pe": _Eng(self, "pe", nc.tensor),
            "act": _Eng(self, "act", nc.scalar),
            "dve": _Eng(self, "dve", nc.vector),
            "pool": _Eng(self, "pool", nc.gpsimd),
            "sp": _Eng(self, "sp", nc.sync),
        }

    def new_sem(self, name):
        return self.stack.enter_context(self.nc.semaphore(name))

    def sb(self, name, shape, dt):
        return self.stack.enter_context(self.nc.sbuf_tensor("sb_" + name, shape, dt))

    def ps(self, name, shape, dt):
        return self.stack.enter_context(self.nc.psum_tensor("ps_" + name, shape, dt))

    def _need(self, reads, writes):
        need = {}

        def req(d):
            for s, v in d.items():
                if need.get(s, 0) < v:
                    need[s] = v

        for k in reads:
            st = self.bufs.get(k)
            if st:
                req(st["w"])
        for k in writes:
            st = self.bufs.get(k)
            if st:
                req(st["w"])
                req(st["r"])
        return need

    def _waits(self, E, need):
        for s, v in need.items():
            if E.name == "pe" and s is E.sem:
                continue
            if E.seen.get(s, 0) >= v:
                continue
            E.h.wait_ge(s, v)
            E.seen[s] = v

    def _record(self, reads, writes, sem, val):
        for k in reads:
            st = self.bufs.setdefault(k, {"w": {}, "r": {}})
            if st["r"].get(sem, 0) < val:
                st["r"][sem] = val
        for k in writes:
            st = self.bufs.setdefault(k, {"w": {}, "r": {}})
            if st["w"].get(sem, 0) < val:
                st["w"][sem] = val

    def op(self, en, fn, reads=(), writes=(), signal=True):
        E = self.eng[en]
        if signal and not E.in_chain:
            E.rotate()
        E.in_chain = not signal
        self._waits(E, self._need(reads, writes))
        ins = fn(E.h)
        sem, val = E.sem, E.cnt + 1
        if signal:
            ins.then_inc(E.sem, 1)
            E.cnt += 1
        self._record(reads, writes, sem, val)
        return ins

    def dma(self, q, out, in_, reads=(), writes=(), skey=None):
        E = self.eng[q]
        self._waits(E, self._need(reads, writes))
        if skey not in self.dsem:
            self.dsem[skey] = [self.new_sem(f"d_{len(self.dsem)}"), 0]
        ds = self.dsem[skey]
        ins = E.h.dma_start(out=out, in_=in_)
        ds[1] += 16
        ins.then_inc(ds[0], 16)
        self._record(reads, writes, ds[0], ds[1])
        return ins

    def barrier(self, engs=("pe", "act", "dve")):
        for a in engs:
            A = self.eng[a]
            need = {}
            for b in engs:
                if b == a:
                    continue
                B = self.eng[b]
                if B.cnt > 0:
                    need[B.sem] = B.cnt
            for k, (ds, dv) in self.dsem.items():
                if not (isinstance(k, tuple) and k[0] == "w") and dv > 0:
                    need[ds] = dv
            self._waits(A, need)

    def finish(self, en="sp"):
        E = self.eng[en]
        need = {}
        for s, v in self.dsem.values():
            need[s] = v
        for b in self.eng.values():
            if b is not E and b.cnt > 0:
                need[b.sem] = b.cnt
        self._waits(E, need)


def build_program(n_layers=DEPTH, passes=(0, 1), dbg=None):
    nc = bass.Bass("TRN2", target_bir_lowering=False)
    dr = {}

    def din(name, shape):
        dr[name] = nc.dram_tensor(name, list(shape), F32, kind="ExternalInput").ap()
        return dr[name]

    def dout(name, shape):
        dr[name] = nc.dram_tensor(name, list(shape), F32, kind="ExternalOutput").ap()
        return dr[name]

    TB = 256
    xT_in = [din("xT_a", (D, T)), din("xT_b", (D, TB))]
    flag_d = din("flag", (128, 2))
    st_in = din("st_in", (DEPTH, 2, 128, 1024))
    condT = din("condT", (128, 16))
    pp_d = din("pp", (128, DEPTH * PPL))
    cc_d = din("cc", (128, CC_N))
    pos_d = din("pos", (128, 2050))
    w_mod = din("w_mod", (DEPTH, D, 9 * D))
    w_ffn_in = din("w_ffn_in", (DEPTH, 2, D, 2 * DFF))
    w_ffn_out = din("w_ffn_out", (DEPTH, 2, DFF, D))
    w_in = din("w_in", (DEPTH, D, 7712))
    w_br_ssd = din("w_br_ssd", (DEPTH, 1024, D))
    w_br_conf = din("w_br_conf", (DEPTH, 512, D))
    w_br_pool = din("w_br_pool", (DEPTH, 512, D))
    pool_w = din("pool_w", (DEPTH, 4, 128, 128))
    w_out = din("w_out", (DEPTH, D, D))
    yT_out = [dout("yT_a", (D, T)), dout("yT_b", (D, TB))]
    st_out = dout("st_out", (5, DEPTH, 2, 128, 1024))
    wscrs = [nc.dram_tensor(f"wscr{l}", [70, 128, 4096], BF16, kind="Internal").ap() for l in range(DEPTH)]
    dbg_out = {}
    if dbg:
        for k, shp in dbg.items():
            dbg_out[k] = dout("dbg_" + k, shp)

    with ExitStack() as stack:
        kb = KB(nc, stack)
        R_X = kb.sb("R_X", [128, 8192], F32)
        R_HY = kb.sb("R_HY", [128, 8192], F32)
        R_A = kb.sb("R_A", [128, 11264], F32)
        R_MX = kb.sb("R_MX", [128, 12288], F32)
        NSLOT = 2
        slots = [kb.sb(f"wslot{i}", [128, 4096], BF16) for i in range(NSLOT)]
        ppt = [kb.sb(f"ppt{i}", [128, PPL], F32) for i in range(2)]
        mcall = kb.sb("mcall", [128, 2 * DEPTH * 72], F32)
        rsegs = [kb.sb(f"rseg{i}", [128, 4, 128], F32) for i in range(2)]
        dgb = [kb.sb(f"dgb{i}", [128, 128], BF16) for i in range(4)]
        cc = kb.sb("cc", [128, CC_N], F32)
        cb = kb.sb("cb", [128, 512], BF16)
        onesf = kb.sb("onesf", [128, 128], F32)
        condsb = kb.sb("condsb", [128, 16], F32)
        flagsb = kb.sb("flagsb", [128, 2], F32)
        scond = kb.sb("scond", [128, 8, 2], BF16)
        modT = kb.sb("modT", [128, 72], F32)
        rstd = [kb.sb(f"rstd{i}", [128, 512], F32) for i in range(2)]
        tmpf = [kb.sb(f"tmpf{i}", [128, 512], F32) for i in range(3)]
        sqb = [kb.sb(f"sqb{i}", [128, 512], BF16) for i in range(2)]
        sst = kb.sb("sst", [128, 512], F32)
        decb = [kb.sb(f"decb{i}", [128, 4, 128], BF16) for i in range(2)]
        mtb = [kb.sb(f"mtb{i}", [128, 4, 128], BF16) for i in range(2)]
        lay = kb.sb("lay", [128, 128], F32)
        dtt = kb.sb("dtt", [128, NCH, 32], F32)
        adt = kb.sb("adt", [128, NCH, 32], F32)
        pb = [kb.ps(f"pb{i}", [128, 512], F32) for i in range(7)]
        pbt = kb.ps("pbt", [128, 1024], BF16)

        xres = R_X[:, :].rearrange("p (k t) -> p k t", k=8)
        hbuf = R_HY[:, 0:4096].bitcast(BF16).rearrange("p (k t) -> p k t", k=8)
        zs = R_HY[:, 4096:8192].bitcast(BF16).rearrange("p (c f) -> p c f", c=8)
        ybuf = R_HY[:, :].rearrange("p (k t) -> p k t", k=8)
        mgb16 = R_HY[:, 4096:8192].bitcast(BF16).rearrange("p (k t) -> p k t", k=8)
        abuf = R_A[:, :].bitcast(BF16).rearrange("p (k t) -> p k t", k=22)
        merged = R_A[:, 0:8192].rearrange("p (k t) -> p k t", k=8)
        Sst = [R_A[:, 8192:9216], R_A[:, 9216:10240]]
        Sbb = R_A[:, 10240:10752].bitcast(BF16)
        GMv = [R_A[:, 10752:11008].bitcast(BF16).rearrange("p (g l) -> p g l", g=4),
               R_A[:, 11008:11264].bitcast(BF16).rearrange("p (g l) -> p g l", g=4)]
        x_tok = R_HY[:, 0:512].bitcast(BF16)
        B_toks = [R_HY[:, 512:768].bitcast(BF16), R_HY[:, 3840:4096].bitcast(BF16)]
        xdt = [R_HY[:, 768:1280].bitcast(BF16), R_HY[:, 1280:1792].bitcast(BF16)]
        xdte = R_HY[:, 1792:2304].bitcast(BF16)
        ysb = R_HY[:, 2304:3328]
        ttb = R_HY[:, 3328:3840].bitcast(BF16)
        xs_fm = R_MX[:, 0:4096].bitcast(BF16).rearrange("p (k t) -> p k t", k=8)
        B_fm = R_MX[:, 4096:6144].bitcast(BF16).rearrange("p (k t) -> p k t", k=4)
        C_fm = R_MX[:, 6144:8192].bitcast(BF16).rearrange("p (k t) -> p k t", k=4)
        Sf_all = R_MX[:, 8192:12288].bitcast(BF16).rearrange("p (c f) -> p c f", c=8)

        U_f, Lo_f = cc[:, CC_U:CC_U + 128], cc[:, CC_LO:CC_LO + 128]
        SU_f, SL_f = cc[:, CC_SU:CC_SU + 128], cc[:, CC_SL:CC_SL + 128]
        id_f = cc[:, CC_ID:CC_ID + 128]
        U_b, Lo_b, id_b, ones_b = cb[:, 0:128], cb[:, 128:256], cb[:, 256:384], cb[:, 384:512]

        ALLX = [("x", tt) for tt in range(2)]
        ALLH = [("h", tt) for tt in range(2)]
        ALLZS = [("zs", c) for c in range(8)]

        P = {"T": 1024, "TW": 512, "NT": 2, "NCH": 8}

        def tsl(tt):
            return slice(tt * P["TW"], (tt + 1) * P["TW"])

        def fsl(hf):
            return slice(hf * 512, (hf + 1) * 512)

        def csl(c):
            return slice(c * 128, (c + 1) * 128)

        kb.dma("sp", cc[:], cc_d, writes=["cc"], skey="c1")
        kb.dma("sp", condsb[:], condT, writes=["cond"], skey="c2")
        kb.dma("sp", flagsb[:], flag_d, writes=["flag"], skey="c3")
        fcol = flagsb[:, 0:1]
        kb.op("dve", lambda e: e.tensor_copy(out=cb[:, 0:256], in_=cc[:, 0:256]), reads=["cc"], writes=["cb"])
        kb.op("dve", lambda e: e.tensor_copy(out=cb[:, 256:384], in_=id_f), reads=["cc"], writes=["cb"])
        kb.op("dve", lambda e: e.memset(cb[:, 384:512], 1.0), writes=["cb"])
        kb.op("dve", lambda e: e.memset(onesf[:], 1.0), writes=["onesf"])

        wctr = [0]

        wcache = {"n": 0, "mode": "fill", "count": [0, 0]}

        def wload(parts, cache=True):
            s = wctr[0] % NSLOT
            wctr[0] += 1
            key = ("w", s)
            if cache and wcache["mode"] == "use":
                n = wcache["n"]
                wcache["n"] += 1
                ly = wcache["l"]
                kb.dma("sp", slots[s][:, 0:4096], wscrs[ly][n], reads=[("wscr", ly, n)], writes=[key], skey=key)
                return slots[s], key
            for dst, src in parts:
                kb.dma("pool", dst(slots[s]), src, writes=[key], skey=key)
            if cache and wcache["mode"] == "fill":
                n = wcache["n"]
                wcache["n"] += 1
                ly = wcache["l"]
                kb.dma("sp", wscrs[ly][n], slots[s][:, 0:4096], reads=[key], writes=[("wscr", ly, n)], skey=("wout", s))
            return slots[s], key

        def wblock(wmat, r0, kc_n, c0, ncols, cache=True):
            src = wmat[r0:r0 + kc_n * 128, c0:c0 + ncols].rearrange("(kc p) m -> p kc m", p=128)
            n = kc_n * ncols
            sl, key = wload([(lambda s: s[:, 0:n].rearrange("p (kc m) -> p kc m", kc=kc_n), src)], cache=cache)
            return sl[:, 0:n].rearrange("p (kc m) -> p kc m", kc=kc_n), key

        bankctr = [0]

        def nbank():
            b = bankctr[0] % 5
            bankctr[0] += 1
            return b

        tctr = [0]

        def ntmp():
            i = tctr[0] % 3
            tctr[0] += 1
            return i

        def dump(name, ap, keys):
            if name in dbg_out:
                kb.dma("sp", dbg_out[name], ap, reads=keys, skey=("dbg", name))

        def ppc(l, off, n=1):
            return ppt[l % 2][:, off: off + n]

        cur = {"mc": None}

        def mcf(c0, n=1):
            return cur["mc"][:, c0:c0 + n]

        def rms_stats(src_fn, src_keys_fn, KC, dim, tt):
            for kc in range(KC):
                q = kc % 2
                kb.op("act", lambda e: e.activation(out=sqb[q][:, 0:P["TW"]], in_=src_fn(kc, tt), func=AF.Square),
                      reads=src_keys_fn(kc, tt), writes=[("sqb", q)])
                kb.op("pe", lambda e: e.matmul(pb[6][:, 0:P["TW"]], lhsT=ones_b, rhs=sqb[q][:, 0:P["TW"]], start=(kc == 0),
                                               stop=(kc == KC - 1)),
                      reads=[("sqb", q), "cb"], writes=[("pb", 6)], signal=True)
            kb.op("act", lambda e: e.activation(out=rstd[tt][:, 0:P["TW"]], in_=pb[6][:, 0:P["TW"]], func=AF.Ln, bias=EPS,
                                                scale=1.0 / dim),
                  reads=[("pb", 6)], writes=[("rstd", tt)])
            kb.op("act", lambda e: e.activation(out=rstd[tt][:, 0:P["TW"]], in_=rstd[tt][:, 0:P["TW"]], func=AF.Exp, scale=-0.5),
                  reads=[("rstd", tt)], writes=[("rstd", tt)])

        def fin_rstd(tt, bank, dim):
            kb.op("act", lambda e: e.activation(out=rstd[tt][:, 0:P["TW"]], in_=pb[bank][:, 0:P["TW"]], func=AF.Ln, bias=EPS,
                                                scale=1.0 / dim),
                  reads=[("pb", bank)], writes=[("rstd", tt)])
            kb.op("act", lambda e: e.activation(out=rstd[tt][:, 0:P["TW"]], in_=rstd[tt][:, 0:P["TW"]], func=AF.Exp, scale=-0.5),
                  reads=[("rstd", tt)], writes=[("rstd", tt)])

        SBANK = (6, 5)
        sqc = [0]

        class FusedStats:
            def __init__(self, nsteps):
                self.pend = None
                self.n = nsteps
                self.cnt = [0, 0]

            def _emit(self, p):
                q, tt = p
                i = self.cnt[tt]
                self.cnt[tt] += 1
                kb.op("pe", lambda e: e.matmul(pb[SBANK[tt]][:, 0:P["TW"]], lhsT=ones_b, rhs=sqb[q][:, 0:P["TW"]], start=(i == 0),
                                               stop=(i == self.n - 1)),
                      reads=[("sqb", q), "cb"], writes=[("pb", SBANK[tt])], signal=True)

            def add(self, b, tt):
                self.add_src(pb[b][:, 0:P["TW"]], [("pb", b)], tt)

            def add_src(self, src, keys, tt):
                q = sqc[0] % 2
                sqc[0] += 1
                kb.op("act", lambda e: e.activation(out=sqb[q][:, 0:P["TW"]], in_=src, func=AF.Square),
                      reads=keys, writes=[("sqb", q)])
                if self.pend:
                    self._emit(self.pend)
                self.pend = (q, tt)

            def finish(self, dim):
                self._emit(self.pend)
                for tt in range(P["NT"]):
                    fin_rstd(tt, SBANK[tt], dim)

        xstat = {"ok": False}

        def make_h(ai, stash=None, use_stash=None):
            have = xstat["ok"] or (use_stash is not None)
            xstat["ok"] = False
            for tt in range(P["NT"]):
                if not have:
                    rms_stats(lambda kc, t_: xres[:, kc, tsl(t_)], lambda kc, t_: [("x", t_)], 8, D, tt)
                rs_ap, rs_key = rstd[tt][:, 0:P["TW"]], ("rstd", tt)
                if use_stash is not None:
                    rs_ap, rs_key = use_stash[tt][:, 0:P["TW"]], ("rstash", tt)
                if stash is not None:
                    kb.op("act", lambda e: e.copy(out=stash[tt][:, 0:P["TW"]], in_=rs_ap), reads=[rs_key],
                          writes=[("rstash", tt)])
                for kc in range(8):
                    i = ntmp()
                    kb.op("dve", lambda e: e.tensor_tensor(out=tmpf[i][:, 0:P["TW"]], in0=xres[:, kc, tsl(tt)],
                                                           in1=rs_ap, op=ALU.mult),
                          reads=[("x", tt), rs_key], writes=[("tmpf", i)])
                    kb.op("act", lambda e: e.activation(out=hbuf[:, kc, tsl(tt)], in_=tmpf[i][:, 0:P["TW"]],
                                                        func=AF.Identity,
                                                        bias=mcf((ai + 1) * 8 + kc),
                                                        scale=mcf(ai * 8 + kc)),
                          reads=[("tmpf", i), "mcoef"], writes=[("h", tt)])

        def resid_update(ci, src_fn, src_keys_fn):
            fsx = FusedStats(8)
            for tt in range(P["NT"]):
                for mo in range(8):
                    i = ntmp()
                    kb.op("dve", lambda e: e.tensor_tensor(out=tmpf[i][:, 0:P["TW"]], in0=src_fn(mo, tt), in1=rstd[tt][:, 0:P["TW"]],
                                                           op=ALU.mult),
                          reads=src_keys_fn(mo, tt) + [("rstd", tt)], writes=[("tmpf", i)])
                    kb.op("dve", lambda e: e.scalar_tensor_tensor(out=xres[:, mo, tsl(tt)], in0=tmpf[i][:, 0:P["TW"]],
                                                                  scalar=mcf(ci * 8 + mo),
                                                                  in1=xres[:, mo, tsl(tt)], op0=ALU.mult,
                                                                  op1=ALU.add),
                          reads=[("tmpf", i), ("x", tt), "mcoef"], writes=[("x", tt)])
                    fsx.add_src(xres[:, mo, tsl(tt)], [("x", tt)], tt)
            fsx.finish(D)
            xstat["ok"] = True

        def mod_steps(l):
            for blk in range(18):
                wv, wk = wblock(w_mod[l], 0, 8, blk * 512, 512, cache=False)
                for mc in range(4):
                    j = blk * 4 + mc
                    for kc in range(8):
                        kb.op("pe", lambda e: e.matmul(pb[6][:, 2 * j:2 * j + 2], lhsT=wv[:, kc, csl(mc)],
                                                       rhs=scond[:, kc, :], start=(kc == 0), stop=(kc == 7)),
                              reads=[wk, "scond"], writes=[("pb", 6)], signal=(kc == 7))
                yield
            pv = pb[6][:, 0:144].rearrange("p (j c) -> p j c", c=2)
            for ci in range(2):
                mco = mcall[:, (ci * DEPTH + l) * 72:(ci * DEPTH + l + 1) * 72]
                kb.op("dve", lambda e: e.tensor_tensor(out=modT[:, :], in0=pv[:, :, ci], in1=ppc(l, PP_BMOD, 72),
                                                       op=ALU.add),
                      reads=[("pb", 6), "pp"], writes=["modT"])
                for s_ in range(3):
                    sh, sc, g = (modT[:, (3 * s_ + q) * 8:(3 * s_ + q) * 8 + 8] for q in range(3))
                    nga = ppc(l, PP_NG + (2 * s_) * 8, 8)
                    ngb = ppc(l, PP_NG + (2 * s_ + 1) * 8, 8)
                    kb.op("dve", lambda e: e.scalar_tensor_tensor(out=mco[:, (3 * s_) * 8:(3 * s_) * 8 + 8], in0=sc,
                                                                  scalar=1.0, in1=nga, op0=ALU.add, op1=ALU.mult),
                          reads=["modT", "pp"], writes=["mcoef"])
                    kb.op("dve", lambda e: e.tensor_copy(out=mco[:, (3 * s_ + 1) * 8:(3 * s_ + 1) * 8 + 8], in_=sh),
                          reads=["modT"], writes=["mcoef"])
                    kb.op("dve", lambda e: e.scalar_tensor_tensor(out=mco[:, (3 * s_ + 2) * 8:(3 * s_ + 2) * 8 + 8],
                                                                  in0=g, scalar=(1.0 if s_ == 1 else 0.5), in1=ngb,
                                                                  op0=ALU.mult, op1=ALU.mult),
                          reads=["modT", "pp"], writes=["mcoef"])
            yield

        bg = {"gen": None}

        def bg_step(n=1):
            for _ in range(n):
                if bg["gen"] is not None:
                    try:
                        next(bg["gen"])
                    except StopIteration:
                        bg["gen"] = None

        def bg_drain():
            while bg["gen"] is not None:
                bg_step()

        def ffn(l, which):
            s3 = 0 if which == 0 else 2
            make_h(3 * s3)
            dump(f"h{which}", hbuf[:, :, :], ALLH)
            wi = w_ffn_in[l, which]
            for j in range(11):
                srcg = wi[:, j * 256:(j + 1) * 256].rearrange("(kc p) m -> p kc m", p=128)
                srcu = wi[:, DFF + j * 256:DFF + (j + 1) * 256].rearrange("(kc p) m -> p kc m", p=128)
                sl, wk = wload([
                    (lambda s: s[:, 0:2048].rearrange("p (kc m) -> p kc m", kc=8), srcg),
                    (lambda s: s[:, 2048:4096].rearrange("p (kc m) -> p kc m", kc=8), srcu)])
                wg = sl[:, 0:2048].rearrange("p (kc m) -> p kc m", kc=8)
                wu = sl[:, 2048:4096].rearrange("p (kc m) -> p kc m", kc=8)
                for tt in range(P["NT"]):
                    for mc in range(2):
                        bg, bu = nbank(), nbank()
                        for kc in range(8):
                            kb.op("pe", lambda e: e.matmul(pb[bg][:, 0:P["TW"]], lhsT=wg[:, kc, csl(mc)],
                                                           rhs=hbuf[:, kc, tsl(tt)], start=(kc == 0), stop=(kc == 7)),
                                  reads=[wk, ("h", tt)], writes=[("pb", bg)], signal=(kc == 7))
                        for kc in range(8):
                            kb.op("pe", lambda e: e.matmul(pb[bu][:, 0:P["TW"]], lhsT=wu[:, kc, csl(mc)],
                                                           rhs=hbuf[:, kc, tsl(tt)], start=(kc == 0), stop=(kc == 7)),
                                  reads=[wk, ("h", tt)], writes=[("pb", bu)], signal=(kc == 7))
                        i = ntmp()
                        kb.op("act", lambda e: e.activation(out=tmpf[i][:, 0:P["TW"]], in_=pb[bg][:, 0:P["TW"]], func=AF.Silu),
                              reads=[("pb", bg)], writes=[("tmpf", i)])
                        kb.op("dve", lambda e: e.tensor_tensor(out=abuf[:, 2 * j + mc, tsl(tt)], in0=tmpf[i][:, 0:P["TW"]],
                                                               in1=pb[bu][:, 0:P["TW"]], op=ALU.mult),
                              reads=[("tmpf", i), ("pb", bu)], writes=[("a", tt)])
            wo = w_ffn_out[l, which]
            fs = FusedStats(8)
            for mo in range(8):
                srcw = wo[:, mo * 128:(mo + 1) * 128].rearrange("(kc p) m -> p kc m", p=128)
                sl, wk = wload([(lambda s_: s_[:, 0:2816].rearrange("p (kc m) -> p kc m", kc=22), srcw)])
                wv = sl[:, 0:2816].rearrange("p (kc m) -> p kc m", kc=22)
                for tt in range(P["NT"]):
                    b = nbank()
                    for kc in range(22):
                        kb.op("pe", lambda e: e.matmul(pb[b][:, 0:P["TW"]], lhsT=wv[:, kc, :], rhs=abuf[:, kc, tsl(tt)],
                                                       start=(kc == 0), stop=(kc == 21)),
                              reads=[wk, ("a", tt)], writes=[("pb", b)], signal=(kc == 21))
                    kb.op("act", lambda e: e.copy(out=ybuf[:, mo, tsl(tt)], in_=pb[b][:, 0:P["TW"]]),
                          reads=[("pb", b)], writes=[("y", tt)])
                    fs.add(b, tt)
            fs.finish(D)
            resid_update(3 * s3 + 2, lambda mo, t_: ybuf[:, mo, tsl(t_)], lambda mo, t_: [("y", t_)])
            dump(f"x_ffn{which}", xres[:, :, :], ALLX)

        def dwconv(pad3, acc3, wcols, K, keys_pad, keys_acc, center_done=False):
            L = acc3.shape[2]
            if not center_done:
                kb.op("dve", lambda e: e.tensor_scalar(out=acc3, in0=pad3[:, :, 0:L], scalar1=wcols[:, 0:1], scalar2=None,
                                                       op0=ALU.mult),
                      reads=keys_pad + ["pp"], writes=keys_acc)
            for k in range(0 if center_done else 1, K):
                if center_done and k == K // 2:
                    continue
                kb.op("dve", lambda e: e.scalar_tensor_tensor(out=acc3, in0=pad3[:, :, k:k + L],
                                                              scalar=wcols[:, k:k + 1], in1=acc3, op0=ALU.mult,
                                                              op1=ALU.add),
                      reads=keys_pad + keys_acc + ["pp"], writes=keys_acc)

        def mixer(l, ps_id):
            nseq, L = (4, 256) if ps_id == 0 else (1, 256)
            cpl = L // 128
            wi = w_in[l]
            rstash = [R_A[:, 5000:5512], R_A[:, 5512:6024]]
            make_h(3, stash=rstash)
            dump("h_mix", hbuf[:, :, :], ALLH)
            kb.op("act", lambda e: e.activation(out=lay[:, 0:32], in_=ppc(l, PP_ALOG, 32), func=AF.Exp),
                  reads=["pp"], writes=["lay"])
            kb.op("dve", lambda e: e.tensor_scalar(out=lay[:, 0:32], in0=lay[:, 0:32], scalar1=-1.0, scalar2=None,
                                                   op0=ALU.mult), reads=["lay"], writes=["lay"])
            PW = L + 4
            padvs = [R_A[:, i * 1100:i * 1100 + nseq * PW].rearrange("p (s t) -> p s t", s=nseq) for i in range(2)]
            accfl = [R_A[:, 2300 + i * 1024:2300 + i * 1024 + P["T"]] for i in range(2)]
            kb.op("dve", lambda e: e.memset(R_A[:, 0:2200], 0.0), writes=[("cpad", 0), ("cpad", 1)])
            def xbc_tail(ch):
                dwconv(padvs[ch % 2], accfl[ch % 2].rearrange("p (s t) -> p s t", s=nseq), ppc(l, PP_SCW + ch * 5, 5), 5,
                       [("cpad", ch % 2)], [("xacc", ch % 2)], center_done=True)
                if ch < 8:
                    dstv = xs_fm[:, ch, 0:P["T"]]
                elif ch < 12:
                    dstv = B_fm[:, ch - 8, 0:P["T"]]
                else:
                    dstv = C_fm[:, ch - 12, 0:P["T"]]
                kb.op("act", lambda e: e.activation(out=dstv, in_=accfl[ch % 2], func=AF.Silu,
                                                    bias=ppc(l, PP_SCB + ch, 1), scale=1.0),
                      reads=[("xacc", ch % 2), "pp"], writes=["mx"])

            for blk in range(4):
                wv, wk = wblock(wi, 0, 8, C_XBC + blk * 512, 512)
                for mc in range(4):
                    ch = blk * 4 + mc
                    padv = padvs[ch % 2]
                    for tt in range(P["NT"]):
                        b = nbank()
                        for kc in range(8):
                            kb.op("pe", lambda e: e.matmul(pb[b][:, 0:P["TW"]], lhsT=wv[:, kc, csl(mc)], rhs=hbuf[:, kc, tsl(tt)],
                                                           start=(kc == 0), stop=(kc == 7)),
                                  reads=[wk, ("h", tt)], writes=[("pb", b)], signal=(kc == 7))
                        spt = P["TW"] // L
                        if L <= P["TW"]:
                            dst = padv[:, tt * spt:(tt + 1) * spt, 2:2 + L]
                            src = pb[b][:, 0:P["TW"]].rearrange("p (s t) -> p s t", s=spt)
                        else:
                            dst = padv[:, 0, 2 + tt * 512:2 + (tt + 1) * 512]
                            src = pb[b][:, 0:P["TW"]]
                        kb.op("act", lambda e: e.copy(out=dst, in_=src), reads=[("pb", b)], writes=[("cpad", ch % 2)])
                        kb.op("act", lambda e: e.activation(out=accfl[ch % 2][:, tsl(tt)], in_=pb[b][:, 0:P["TW"]],
                                                            func=AF.Identity, scale=ppc(l, PP_SCW + ch * 5 + 2, 1)),
                              reads=[("pb", b), "pp"], writes=[("xacc", ch % 2)])
                    if nseq > 1:
                        kb.op("dve", lambda e: e.tensor_scalar(out=padv[:, 1:nseq, 0:2], in0=padv[:, 0:nseq - 1, L:L + 2],
                                                               scalar1=fcol, scalar2=None, op0=ALU.mult),
                              reads=[("cpad", ch % 2), "flag"], writes=[("cpad", ch % 2)])
                        kb.op("dve", lambda e: e.tensor_scalar(out=padv[:, 0:nseq - 1, L + 2:L + 4], in0=padv[:, 1:nseq, 2:4],
                                                               scalar1=fcol, scalar2=None, op0=ALU.mult),
                              reads=[("cpad", ch % 2), "flag"], writes=[("cpad", ch % 2)])
                    if ch >= 1:
                        xbc_tail(ch - 1)
            xbc_tail(15)
            dump("xs_fm", xs_fm[:, :, :], ["mx"])
            dump("B_fm", B_fm[:, :, :], ["mx"])
            wv, wk = wblock(wi, 0, 8, C_DT, 32)
            for c in range(P["NCH"]):
                for kc in range(8):
                    kb.op("pe", lambda e: e.matmul(pb[6][:, c * 32:(c + 1) * 32], lhsT=hbuf[:, kc, csl(c)],
                                                   rhs=wv[:, kc, :], start=(kc == 0), stop=(kc == 7)),
                          reads=[wk, ("h", c // (P["TW"] // 128))], writes=[("pb", 6)], signal=(kc == 7))
            dt3 = dtt[:, 0:P["NCH"], :]
            kb.op("dve", lambda e: e.tensor_tensor(out=dt3, in0=pb[6][:, 0:32 * P["NCH"]].rearrange("p (c h) -> p c h", c=P["NCH"]),
                                                   in1=ppc(l, PP_DTB, 32).unsqueeze(1).to_broadcast([128, P["NCH"], 32]),
                                                   op=ALU.add),
                  reads=[("pb", 6), "pp"], writes=["dtt"])
            kb.op("act", lambda e: e.activation(out=dt3, in_=dt3, func=AF.Exp), reads=["dtt"], writes=["dtt"])
            kb.op("act", lambda e: e.activation(out=dt3, in_=dt3, func=AF.Ln, bias=1.0, scale=1.0),
                  reads=["dtt"], writes=["dtt"])
            kb.op("dve", lambda e: e.tensor_tensor(out=adt[:, 0:P["NCH"], :], in0=dt3,
                                                   in1=lay[:, 0:32].unsqueeze(1).to_broadcast([128, P["NCH"], 32]),
                                                   op=ALU.mult),
                  reads=["dtt", "lay"], writes=["adt"])
            dump("dtt", dtt[:, :, :], ["dtt"])
            for half in range(2):
                wv, wk = wblock(wi, 0, 8, C_Z + half * 512, 512)
                for c in range(P["NCH"]):
                    b = nbank()
                    for kc in range(8):
                        kb.op("pe", lambda e: e.matmul(pb[b][:, 0:512], lhsT=hbuf[:, kc, csl(c)], rhs=wv[:, kc, :],
                                                       start=(kc == 0), stop=(kc == 7)),
                              reads=[wk, ("h", c // (P["TW"] // 128))], writes=[("pb", b)], signal=(kc == 7))
                    kb.op("act", lambda e: e.activation(out=zs[:, c, fsl(half)], in_=pb[b][:, 0:512], func=AF.Silu),
                          reads=[("pb", b)], writes=[("zs", c)])
            Dbc = ppc(l, PP_DD, 16)

            par = {"p": 0}

            def sstv(pr):
                return sst[:, pr * 256:pr * 256 + 96]

            def prep(c, dirs, pr):
                for k in range(8):
                    kb.op("pe", lambda e: e.transpose(pbt[:, k * 128:(k + 1) * 128], xs_fm[:, k, csl(c)], id_b),
                          reads=["mx", "cb"], writes=["pbt"], signal=(k == 7))
                kb.op("act", lambda e: e.copy(out=x_tok, in_=pbt[:, :]), reads=["pbt"], writes=["x_tok"])
                for g in range(4):
                    kb.op("pe", lambda e: e.transpose(pbt[:, g * 128:(g + 1) * 128], B_fm[:, g, csl(c)], id_b),
                          reads=["mx", "cb", "x_tok"], writes=["pbt"], signal=(g == 3))
                kb.op("act", lambda e: e.copy(out=B_toks[pr], in_=pbt[:, 0:512]), reads=["pbt"], writes=[("B_tok", pr)])
                for d in dirs:
                    a_d = adt[:, c, d * 16:(d + 1) * 16]
                    mats = (U_f, SL_f, onesf[:, :]) if d == 0 else (Lo_f, SU_f, onesf[:, :])
                    for q, m in enumerate(mats):
                        kb.op("pe", lambda e: e.matmul(pb[5][:, d * 48 + q * 16:d * 48 + q * 16 + 16], lhsT=m,
                                                       rhs=a_d, start=True, stop=True),
                              reads=["cc", "onesf", "adt"], writes=[("pb", 5)], signal=True)
                ncol = 48 * len(dirs)
                kb.op("act", lambda e: e.activation(out=sstv(pr)[:, 0:ncol], in_=pb[5][:, 0:ncol], func=AF.Exp),
                      reads=[("pb", 5)], writes=[("sst", pr)])

            def bc16(ap16):
                return ap16.unsqueeze(2).to_broadcast([128, 16, 64])

            def v3(ap):
                return ap.rearrange("p (h q) -> p h q", h=16)

            def su_front(c, d, pr, have_xdt=False):
                if not have_xdt:
                    kb.op("dve", lambda e: e.tensor_tensor(out=v3(xdt[d]), in0=v3(x_tok),
                                                           in1=bc16(dtt[:, c, d * 16:(d + 1) * 16]), op=ALU.mult),
                          reads=["x_tok", "dtt"], writes=[("xdt", d)])
                kb.op("dve", lambda e: e.tensor_tensor(out=v3(xdte), in0=v3(xdt[d]),
                                                       in1=bc16(sstv(pr)[:, d * 48 + 16:d * 48 + 32]), op=ALU.mult),
                      reads=[("xdt", d), ("sst", pr)], writes=["xdte"])

            def su_tail(c, d, pr):
                for hf in range(2):
                    for g2 in range(2):
                        g = hf * 2 + g2
                        kb.op("pe", lambda e: e.matmul(pb[3 + hf][:, g2 * 256:(g2 + 1) * 256],
                                                       lhsT=B_toks[pr][:, g * 128:(g + 1) * 128],
                                                       rhs=xdte[:, g * 256:(g + 1) * 256], start=True, stop=True),
                              reads=[("B_tok", pr), "xdte"], writes=[("pb", 3 + hf)], signal=True)
                kb.op("dve", lambda e: e.tensor_tensor(out=v3(Sst[d]), in0=v3(Sst[d]),
                                                       in1=bc16(sstv(pr)[:, d * 48 + 32:d * 48 + 48]), op=ALU.mult),
                      reads=[("S", d), ("sst", pr)], writes=[("S", d)])
                for hf in range(2):
                    kb.op("dve", lambda e: e.tensor_tensor(out=Sst[d][:, fsl(hf)], in0=Sst[d][:, fsl(hf)],
                                                           in1=pb[3 + hf][:, :], op=ALU.add),
                          reads=[("S", d), ("pb", 3 + hf)], writes=[("S", d)])

            def seq_chunks(s_):
                return [s_ * cpl + i for i in range(cpl)]

            def init_state(d, first):
                if first:
                    if ps_id == 0:
                        kb.dma("act", Sst[d], st_in[l, d], writes=[("S", d)], skey=("stin", d))
                    else:
                        kb.op("dve", lambda e: e.memset(Sst[d], 0.0), writes=[("S", d)])
                else:
                    kb.op("dve", lambda e: e.tensor_scalar(out=Sst[d], in0=Sst[d], scalar1=fcol, scalar2=None,
                                                           op0=ALU.mult),
                          reads=[("S", d), "flag"], writes=[("S", d)])

            so = 0 if ps_id == 0 else 4
            order1 = [c for s_ in range(nseq) for c in seq_chunks(s_)]
            order2 = [c for s_ in reversed(range(nseq)) for c in reversed(seq_chunks(s_))]
            pos1 = {c: i for i, c in enumerate(order1)}
            pos2 = {c: i for i, c in enumerate(order2)}
            prep(order1[0], (0,), 0)
            for s in range(nseq):
                chunks = seq_chunks(s)
                init_state(0, s == 0)
                for c in chunks:
                    i1 = pos1[c]
                    pr = i1 % 2
                    bg_step(1)
                    kb.op("act", lambda e: e.copy(out=Sf_all[:, c, :], in_=Sst[0]), reads=[("S", 0)], writes=["sfall"])
                    su_front(c, 0, pr)
                    if i1 + 1 < len(order1):
                        prep(order1[i1 + 1], (0,), 1 - pr)
                    su_tail(c, 0, pr)
                kb.dma("sp", st_out[so + s, l, 0], Sst[0], reads=[("S", 0)], skey=("sto", 0))
            prep(order2[0], (0, 1), 0)
            for s in reversed(range(nseq)):
                chunks = seq_chunks(s)
                init_state(1, s == nseq - 1)
                for c in reversed(chunks):
                    i2 = pos2[c]
                    pr = i2 % 2
                    bg_step(1)
                    kb.op("act", lambda e: e.copy(out=Sbb, in_=Sst[1]), reads=[("S", 1)], writes=["Sbb"])
                    def front():
                        for d in range(2):
                            kb.op("dve", lambda e: e.tensor_tensor(out=v3(xdt[d]), in0=v3(x_tok),
                                                                   in1=bc16(dtt[:, c, d * 16:(d + 1) * 16]), op=ALU.mult),
                                  reads=["x_tok", "dtt"], writes=[("xdt", d)])
                        for g in range(4):
                            kb.op("pe", lambda e: e.matmul(pb[4][:, g * 128:(g + 1) * 128], lhsT=B_fm[:, g, csl(c)],
                                                           rhs=C_fm[:, g, csl(c)], start=True, stop=True),
                                  reads=["mx"], writes=[("pb", 4)], signal=True)
                        g3 = pb[4][:, :].rearrange("p (g l) -> p g l", g=4)
                        for d, mk in ((0, U_f), (1, Lo_f)):
                            kb.op("dve", lambda e: e.tensor_tensor(out=GMv[d], in0=g3,
                                                                   in1=mk.unsqueeze(1).to_broadcast([128, 4, 128]),
                                                                   op=ALU.mult),
                                  reads=[("pb", 4), "cc"], writes=[("GM", d)])
                        kb.op("dve", lambda e: e.tensor_tensor(out=v3(xdte), in0=v3(x_tok), in1=bc16(Dbc), op=ALU.mult),
                              reads=["x_tok", "pp"], writes=["xdte"])
                        for hf in range(2):
                            kb.op("pe", lambda e: e.matmul(pb[hf][:, :], lhsT=id_b, rhs=xdte[:, fsl(hf)], start=True,
                                                           stop=False),
                                  reads=["xdte", "cb"], writes=[("pb", hf)], signal=True)


                    def stA1(u, cc=None):
                        cc = c if cc is None else cc
                        g, d = divmod(u, 2)
                        mk, lm = (U_f, SL_f) if d == 0 else (Lo_f, SU_f)
                        hh = d * 16 + g * 4
                        sbk = 2 if d == 0 else 5
                        kb.op("dve", lambda e: e.tensor_tensor(
                            out=rsegs[d][:, :, :], in0=mk.unsqueeze(1).to_broadcast([128, 4, 128]),
                            in1=adt[:, cc, hh:hh + 4].unsqueeze(2).to_broadcast([128, 4, 128]), op=ALU.mult),
                            reads=["cc", "adt"], writes=[("rseg", d)])
                        kb.op("pe", lambda e: e.matmul(pb[sbk][:, :], lhsT=lm,
                                                       rhs=rsegs[d][:, :, :].rearrange("p r l -> p (r l)"),
                                                       start=True, stop=True),
                              reads=[("rseg", d), "cc"], writes=[("pb", sbk)], signal=True)

                    def stA2(u):
                        g, d = divmod(u, 2)
                        sbk = 2 if d == 0 else 5
                        kb.op("act", lambda e: e.activation(out=decb[d][:, :, :].rearrange("p r l -> p (r l)"),
                                                            in_=pb[sbk][:, :], func=AF.Exp),
                              reads=[("pb", sbk)], writes=[("dec", d)])

                    def stB(u):
                        g, d = divmod(u, 2)
                        kb.op("dve", lambda e: e.tensor_tensor(out=mtb[d][:, :, :], in0=decb[d][:, :, :],
                                                               in1=GMv[d][:, g:g + 1, :].to_broadcast([128, 4, 128]),
                                                               op=ALU.mult),
                              reads=[("dec", d), ("GM", d)], writes=[("mt", d)])

                    def stC(g):
                        for r in range(4):
                            h = g * 4 + r
                            yb = pb[h // 8]
                            col = (h % 8) * 64
                            for d in range(2):
                                kb.op("pe", lambda e: e.matmul(yb[:, col:col + 64], lhsT=mtb[d][:, r, :],
                                                               rhs=xdt[d][:, h * 64:(h + 1) * 64], start=False,
                                                               stop=(d == 1 and h % 8 == 7)),
                                      reads=[("mt", d), ("xdt", d)], writes=[("pb", h // 8)], signal=True)

                    if i2 == 0:
                        stA1(0); stA1(1)
                    front()
                    stA2(0); stA2(1)
                    for g in range(4):
                        if g < 3:
                            stA1(2 * g + 2); stA1(2 * g + 3)
                        stB(2 * g); stB(2 * g + 1)
                        if g < 3:
                            stA2(2 * g + 2); stA2(2 * g + 3)
                        stC(g)
                    bg_step(2 if i2 < 3 else 1)
                    for d in range(2):
                        sb_src = Sf_all[:, c, :] if d == 0 else Sbb
                        for hf in range(2):
                            for g2 in range(2):
                                g = hf * 2 + g2
                                kb.op("pe", lambda e: e.matmul(pb[3 + hf][:, g2 * 256:(g2 + 1) * 256],
                                                               lhsT=C_fm[:, g, csl(c)],
                                                               rhs=sb_src[:, g * 256:(g + 1) * 256],
                                                               start=True, stop=True),
                                      reads=["mx", "sfall", "Sbb"], writes=[("pb", 3 + hf)], signal=True)
                        for hf in range(2):
                            ev = sstv(pr)[:, d * 48 + hf * 8:d * 48 + hf * 8 + 8].unsqueeze(2).to_broadcast([128, 8, 64])
                            src = pb[3 + hf][:, :].rearrange("p (h q) -> p h q", h=8)
                            dst = ysb[:, fsl(hf)].rearrange("p (h q) -> p h q", h=8)
                            if d == 0:
                                kb.op("dve", lambda e: e.tensor_tensor(out=dst, in0=src, in1=ev, op=ALU.mult),
                                      reads=[("pb", 3 + hf), ("sst", pr)], writes=["ysb"])
                            else:
                                i = ntmp()
                                tv = tmpf[i][:, :].rearrange("p (h q) -> p h q", h=8)
                                kb.op("dve", lambda e: e.tensor_tensor(out=tv, in0=src, in1=ev, op=ALU.mult),
                                      reads=[("pb", 3 + hf), ("sst", pr)], writes=[("tmpf", i)])
                                kb.op("dve", lambda e: e.tensor_tensor(out=ysb[:, fsl(hf)], in0=ysb[:, fsl(hf)],
                                                                       in1=tmpf[i][:, :], op=ALU.add),
                                      reads=[("tmpf", i), "ysb"], writes=["ysb"])
                    if i2 + 1 < len(order2):
                        prep(order2[i2 + 1], (0, 1), 1 - pr)
                        stA1(0, order2[i2 + 1]); stA1(1, order2[i2 + 1])
                    for hf in range(2):
                        kb.op("dve", lambda e: e.tensor_tensor(out=ysb[:, fsl(hf)], in0=ysb[:, fsl(hf)],
                                                               in1=pb[hf][:, :], op=ALU.add),
                              reads=["ysb", ("pb", hf)], writes=["ysb"])
                    if c == 0:
                        dump("ysb0", ysb, ["ysb"])
                    kb.op("dve", lambda e: e.tensor_tensor(out=ttb, in0=ysb, in1=zs[:, c, :], op=ALU.mult),
                          reads=["ysb", ("zs", c)], writes=["ttb"])
                    su_front(c, 1, pr, have_xdt=True)
                    su_tail(c, 1, pr)
                    for k in range(8):
                        kb.op("pe", lambda e: e.transpose(pbt[:, k * 128:(k + 1) * 128], ttb[:, k * 128:(k + 1) * 128],
                                                          id_b),
                              reads=["ttb", "cb"], writes=["pbt"], signal=(k == 7))
                    kb.op("act", lambda e: e.copy(out=xs_fm[:, :, csl(c)],
                                                  in_=pbt[:, :].rearrange("p (k t) -> p k t", k=8)),
                          reads=["pbt"], writes=["mx"])
                kb.dma("sp", st_out[so + s, l, 1], Sst[1], reads=[("S", 1)], skey=("sto", 1))
            bg_drain()
            kb.barrier()
            dump("t_fm", xs_fm[:, :, :], ["mx"])
            make_h(3, use_stash=rstash)

            def merge_branch(bi, wbr, KCb, feat_fn, feat_keys, use_rstd, scale_cols):
                for half in range(2):
                    if KCb == 8:
                        wv, wk = wblock(wbr, 0, 8, half * 512, 512)
                    else:
                        wv, wk = wblock(wbr, 0, 4, half * 512, 512)
                    gv, gk = wblock(wi, 0, 8, C_GATE + bi * 1024 + half * 512, 512)
                    for tt in range(P["NT"]):
                        for mc in range(4):
                            mo = half * 4 + mc
                            bb, bg = nbank(), nbank()
                            for kc in range(KCb):
                                kb.op("pe", lambda e: e.matmul(pb[bb][:, 0:P["TW"]], lhsT=wv[:, kc, csl(mc)], rhs=feat_fn(kc, tt),
                                                               start=(kc == 0), stop=(kc == KCb - 1)),
                                      reads=[wk] + feat_keys, writes=[("pb", bb)], signal=(kc == KCb - 1))
                            for kc in range(8):
                                kb.op("pe", lambda e: e.matmul(pb[bg][:, 0:P["TW"]], lhsT=gv[:, kc, csl(mc)],
                                                               rhs=hbuf[:, kc, tsl(tt)], start=(kc == 0), stop=(kc == 7)),
                                      reads=[gk, ("h", tt)], writes=[("pb", bg)], signal=(kc == 7))
                            i = ntmp()
                            kb.op("act", lambda e: e.activation(out=tmpf[i][:, 0:P["TW"]], in_=pb[bg][:, 0:P["TW"]], func=AF.Sigmoid),
                                  reads=[("pb", bg)], writes=[("tmpf", i)])
                            kb.op("dve", lambda e: e.tensor_tensor(out=tmpf[i][:, 0:P["TW"]], in0=tmpf[i][:, 0:P["TW"]], in1=pb[bb][:, 0:P["TW"]],
                                                                   op=ALU.mult),
                                  reads=[("tmpf", i), ("pb", bb)], writes=[("tmpf", i)])
                            if bi == 0:
                                kb.op("dve", lambda e: e.tensor_tensor(out=merged[:, mo, tsl(tt)], in0=tmpf[i][:, 0:P["TW"]],
                                                                       in1=rstd[tt][:, 0:P["TW"]], op=ALU.mult),
                                      reads=[("tmpf", i), ("rstd", tt)], writes=[("mg", tt)])
                            elif bi == 1:
                                kb.op("dve", lambda e: e.tensor_tensor(out=merged[:, mo, tsl(tt)],
                                                                       in0=merged[:, mo, tsl(tt)], in1=tmpf[i][:, 0:P["TW"]],
                                                                       op=ALU.add),
                                      reads=[("tmpf", i), ("mg", tt)], writes=[("mg", tt)])
                            else:
                                kb.op("dve", lambda e: e.tensor_tensor(out=mgb16[:, mo, tsl(tt)],
                                                                       in0=merged[:, mo, tsl(tt)], in1=tmpf[i][:, 0:P["TW"]],
                                                                       op=ALU.add),
                                      reads=[("tmpf", i), ("mg", tt)], writes=[("mgb", tt)])

            for tt in range(P["NT"]):
                rms_stats(lambda kc, t_: xs_fm[:, kc, tsl(t_)], lambda kc, t_: ["mx"], 8, 1024, tt)
            for kc in range(8):
                kb.op("act", lambda e: e.activation(out=xs_fm[:, kc, 0:P["T"]], in_=xs_fm[:, kc, 0:P["T"]], func=AF.Identity,
                                                    scale=ppc(l, PP_SNG + kc, 1)),
                      reads=["mx", "pp"], writes=["mx"])
            merge_branch(0, w_br_ssd[l], 8, lambda kc, tt: xs_fm[:, kc, tsl(tt)], ["mx"], True, None)
            kb.barrier()
            dump("merged0", merged[:, :, :], [("mg", 0), ("mg", 1)])
            PWc = L + 30
            vpad = R_MX[:, 0:2 * nseq * PWc].bitcast(BF16).rearrange("p (k s t) -> p k s t", k=4, s=nseq)
            cacc = R_MX[:, 4600:4600 + 4096].rearrange("p (k t) -> p k t", k=4)
            cv = R_MX[:, 8700:8700 + 2048].bitcast(BF16).rearrange("p (k t) -> p k t", k=4)
            kb.op("dve", lambda e: e.memset(R_MX[:, 0:2 * nseq * PWc].bitcast(BF16), 0.0), writes=["vpad"])
            for half in range(2):
                wa, ka = wblock(wi, 0, 8, C_CONF + half * 256, 256)
                wg_, kg = wblock(wi, 0, 8, C_CONF + 512 + half * 256, 256)
                for tt in range(P["NT"]):
                    for mc in range(2):
                        ch = half * 2 + mc
                        ba, bg = nbank(), nbank()
                        for kc in range(8):
                            kb.op("pe", lambda e: e.matmul(pb[ba][:, 0:P["TW"]], lhsT=wa[:, kc, csl(mc)], rhs=hbuf[:, kc, tsl(tt)],
                                                           start=(kc == 0), stop=(kc == 7)),
                                  reads=[ka, ("h", tt)], writes=[("pb", ba)], signal=(kc == 7))
                        for kc in range(8):
                            kb.op("pe", lambda e: e.matmul(pb[bg][:, 0:P["TW"]], lhsT=wg_[:, kc, csl(mc)], rhs=hbuf[:, kc, tsl(tt)],
                                                           start=(kc == 0), stop=(kc == 7)),
                                  reads=[kg, ("h", tt)], writes=[("pb", bg)], signal=(kc == 7))
                        i = ntmp()
                        kb.op("act", lambda e: e.activation(out=tmpf[i][:, 0:P["TW"]], in_=pb[bg][:, 0:P["TW"]], func=AF.Sigmoid),
                              reads=[("pb", bg)], writes=[("tmpf", i)])
                        if L <= P["TW"]:
                            spt = P["TW"] // L
                            dst = vpad[:, ch, tt * spt:(tt + 1) * spt, 15:15 + L]
                            s0 = tmpf[i][:, 0:P["TW"]].rearrange("p (s t) -> p s t", s=spt)
                            s1 = pb[ba][:, 0:P["TW"]].rearrange("p (s t) -> p s t", s=spt)
                        else:
                            dst = vpad[:, ch, 0, 15 + tt * 512:15 + (tt + 1) * 512]
                            s0 = tmpf[i][:, 0:P["TW"]]
                            s1 = pb[ba][:, 0:P["TW"]]
                        kb.op("dve", lambda e: e.tensor_tensor(out=dst, in0=s0, in1=s1, op=ALU.mult),
                              reads=[("tmpf", i), ("pb", ba)], writes=["vpad"])
            if nseq > 1:
                for ch in range(4):
                    kb.op("dve", lambda e: e.tensor_scalar(out=vpad[:, ch, 1:nseq, 0:15], in0=vpad[:, ch, 0:nseq - 1, L:L + 15],
                                                           scalar1=fcol, scalar2=None, op0=ALU.mult),
                          reads=["vpad", "flag"], writes=["vpad"])
                    kb.op("dve", lambda e: e.tensor_scalar(out=vpad[:, ch, 0:nseq - 1, L + 15:L + 30],
                                                           in0=vpad[:, ch, 1:nseq, 15:30], scalar1=fcol, scalar2=None,
                                                           op0=ALU.mult),
                          reads=["vpad", "flag"], writes=["vpad"])
            dq = 0
            for ch in range(4):
                banks = [nbank(), nbank()]
                for k in range(31):
                    q = dq % 4
                    dq += 1
                    kb.op("dve", lambda e: e.tensor_scalar(out=dgb[q][:], in0=id_b, scalar1=ppc(l, PP_CCW + ch * 31 + k, 1),
                                                           scalar2=None, op0=ALU.mult),
                          reads=["cb", "pp"], writes=[("dg", q)])
                    for tt in range(P["NT"]):
                        if L <= P["TW"]:
                            spt = P["TW"] // L
                            rhs = vpad[:, ch, tt * spt:(tt + 1) * spt, k:k + L]
                        else:
                            rhs = vpad[:, ch, 0, k + tt * 512:k + (tt + 1) * 512]
                        kb.op("pe", lambda e: e.matmul(pb[banks[tt]][:, 0:P["TW"]], lhsT=dgb[q][:], rhs=rhs, start=(k == 0),
                                                       stop=(k == 30)),
                              reads=[("dg", q), "vpad"], writes=[("pb", banks[tt])], signal=True)
                for tt in range(P["NT"]):
                    kb.op("act", lambda e: e.activation(out=cacc[:, ch, tsl(tt)], in_=pb[banks[tt]][:, 0:P["TW"]], func=AF.Identity,
                                                        bias=ppc(l, PP_CCB + ch, 1), scale=1.0),
                          reads=[("pb", banks[tt]), "pp"], writes=[("cacc", ch)])
            dump("cconv", cacc[:, :, :], [("cacc", ch) for ch in range(4)])
            for tt in range(P["NT"]):
                for ch in range(4):
                    kb.op("pe", lambda e: e.matmul(pb[6][:, 0:P["TW"]], lhsT=onesf[:, :], rhs=cacc[:, ch, tsl(tt)], start=(ch == 0),
                                                   stop=(ch == 3)),
                          reads=[("cacc", ch), "onesf"], writes=[("pb", 6)], signal=True)
                i1, i2 = ntmp(), ntmp()
                for ch in range(4):
                    kb.op("act", lambda e: e.activation(out=tmpf[i1][:, 0:P["TW"]], in_=cacc[:, ch, tsl(tt)], func=AF.Square),
                          reads=[("cacc", ch)], writes=[("tmpf", i1)])
                    kb.op("pe", lambda e: e.matmul(pb[5][:, 0:P["TW"]], lhsT=onesf[:, :], rhs=tmpf[i1][:, 0:P["TW"]], start=(ch == 0),
                                                   stop=(ch == 3)),
                          reads=[("tmpf", i1), "onesf"], writes=[("pb", 5)], signal=True)
                kb.op("dve", lambda e: e.tensor_scalar(out=tmpf[i2][:, 0:P["TW"]], in0=pb[6][:, 0:P["TW"]], scalar1=1.0 / 512, scalar2=None,
                                                       op0=ALU.mult), reads=[("pb", 6)], writes=[("tmpf", i2)])
                kb.op("dve", lambda e: e.tensor_tensor(out=tmpf[i1][:, 0:P["TW"]], in0=tmpf[i2][:, 0:P["TW"]], in1=tmpf[i2][:, 0:P["TW"]], op=ALU.mult),
                      reads=[("tmpf", i2)], writes=[("tmpf", i1)])
                kb.op("dve", lambda e: e.scalar_tensor_tensor(out=rstd[tt][:, 0:P["TW"]], in0=pb[5][:, 0:P["TW"]], scalar=1.0 / 512,
                                                              in1=tmpf[i1][:, 0:P["TW"]], op0=ALU.mult, op1=ALU.subtract),
                      reads=[("pb", 5), ("tmpf", i1)], writes=[("rstd", tt)])
                kb.op("act", lambda e: e.activation(out=rstd[tt][:, 0:P["TW"]], in_=rstd[tt][:, 0:P["TW"]], func=AF.Ln, bias=EPS, scale=1.0),
                      reads=[("rstd", tt)], writes=[("rstd", tt)])
                kb.op("act", lambda e: e.activation(out=rstd[tt][:, 0:P["TW"]], in_=rstd[tt][:, 0:P["TW"]], func=AF.Exp, scale=-0.5),
                      reads=[("rstd", tt)], writes=[("rstd", tt)])
                for ch in range(4):
                    kb.op("dve", lambda e: e.tensor_tensor(out=cacc[:, ch, tsl(tt)], in0=cacc[:, ch, tsl(tt)],
                                                           in1=tmpf[i2][:, 0:P["TW"]], op=ALU.subtract),
                          reads=[("cacc", ch), ("tmpf", i2)], writes=[("cacc", ch)])
                    kb.op("dve", lambda e: e.tensor_tensor(out=cacc[:, ch, tsl(tt)], in0=cacc[:, ch, tsl(tt)],
                                                           in1=rstd[tt][:, 0:P["TW"]], op=ALU.mult),
                          reads=[("cacc", ch), ("rstd", tt)], writes=[("cacc", ch)])
                    kb.op("act", lambda e: e.activation(out=cv[:, ch, tsl(tt)], in_=cacc[:, ch, tsl(tt)], func=AF.Silu,
                                                        bias=ppc(l, PP_CLB + ch, 1), scale=ppc(l, PP_CLG + ch, 1)),
                          reads=[("cacc", ch), "pp"], writes=["cv"])
            merge_branch(1, w_br_conf[l], 4, lambda kc, tt: cv[:, kc, tsl(tt)], ["cv"], False, None)
            kb.barrier()
            PWp = L + 16
            NP_ = nseq * PWp
            upad = R_MX[:, 0:NP_].rearrange("p (s t) -> p s t", s=nseq)
            sA = R_MX[:, 1200:1200 + NP_].rearrange("p (s t) -> p s t", s=nseq)
            sB = R_MX[:, 2400:2400 + NP_].rearrange("p (s t) -> p s t", s=nseq)
            opad = R_MX[:, 3600:3600 + NP_].rearrange("p (s t) -> p s t", s=nseq)
            cA = R_MX[:, 4800:4800 + NP_].rearrange("p (s t) -> p s t", s=nseq)
            cB = R_MX[:, 6000:6000 + NP_].rearrange("p (s t) -> p s t", s=nseq)
            pooled = R_MX[:, 7200:7200 + 2048].bitcast(BF16).rearrange("p (k t) -> p k t", k=4)
            mixed = R_MX[:, 9300:9300 + 2048].bitcast(BF16).rearrange("p (k t) -> p k t", k=4)
            kb.op("dve", lambda e: e.memset(R_MX[:, 0:NP_], 0.0), writes=["upad"])
            kb.op("dve", lambda e: e.memset(R_MX[:, 3600:3600 + NP_], 0.0), writes=["opad"])
            kb.op("dve", lambda e: e.memset(opad[:, :, 8:8 + L], 1.0), writes=["opad"])
            if nseq > 1:
                kb.op("dve", lambda e: e.tensor_scalar(out=opad[:, 1:nseq, 0:8], in0=opad[:, 0:nseq - 1, 8:16], scalar1=fcol,
                                                       scalar2=None, op0=ALU.mult), reads=["opad", "flag"], writes=["opad"])
                kb.op("dve", lambda e: e.tensor_scalar(out=opad[:, 0:nseq - 1, L + 8:L + 16], in0=opad[:, 1:nseq, 8:16],
                                                       scalar1=fcol, scalar2=None, op0=ALU.mult),
                      reads=["opad", "flag"], writes=["opad"])
            wv, wk = wblock(wi, 0, 8, C_POOL, 512)
            pwv, pwk = wload([(lambda s: s[:, 0:512].rearrange("p (g d) -> p g d", g=4),
                               pool_w[l].rearrange("g c d -> c g d"))])
            pwv = pwv[:, 0:512].rearrange("p (g d) -> p g d", g=4)
            for gi in range(4):
                w_ = 2 << gi
                for tt in range(P["NT"]):
                    b = nbank()
                    for kc in range(8):
                        kb.op("pe", lambda e: e.matmul(pb[b][:, 0:P["TW"]], lhsT=wv[:, kc, csl(gi)], rhs=hbuf[:, kc, tsl(tt)],
                                                       start=(kc == 0), stop=(kc == 7)),
                              reads=[wk, ("h", tt)], writes=[("pb", b)], signal=(kc == 7))
                    if L <= P["TW"]:
                        spt = P["TW"] // L
                        dst = upad[:, tt * spt:(tt + 1) * spt, 8:8 + L]
                        src = pb[b][:, 0:P["TW"]].rearrange("p (s t) -> p s t", s=spt)
                    else:
                        dst = upad[:, 0, 8 + tt * 512:8 + (tt + 1) * 512]
                        src = pb[b][:, 0:P["TW"]]
                    kb.op("act", lambda e: e.copy(out=dst, in_=src), reads=[("pb", b)], writes=["upad"])
                if nseq > 1:
                    kb.op("dve", lambda e: e.tensor_scalar(out=upad[:, 1:nseq, 0:8], in0=upad[:, 0:nseq - 1, L:L + 8],
                                                           scalar1=fcol, scalar2=None, op0=ALU.mult),
                          reads=["upad", "flag"], writes=["upad"])
                    kb.op("dve", lambda e: e.tensor_scalar(out=upad[:, 0:nseq - 1, L + 8:L + 16], in0=upad[:, 1:nseq, 8:16],
                                                           scalar1=fcol, scalar2=None, op0=ALU.mult),
                          reads=["upad", "flag"], writes=["upad"])
                for (src0, bA, bB, kn) in ((upad, sA, sB, "ps"), (opad, cA, cB, "pc")):
                    cur, n, step = src0, PWp, 1
                    bufs2 = [bA, bB]
                    bi2 = 0
                    while step < w_:
                        n2 = n - step
                        dstb = bufs2[bi2]
                        kb.op("dve", lambda e: e.tensor_tensor(out=dstb[:, :, 0:n2], in0=cur[:, :, 0:n2],
                                                               in1=cur[:, :, step:step + n2], op=ALU.add),
                              reads=["upad", "opad", (kn, 0), (kn, 1)], writes=[(kn, bi2)])
                        cur, n, step = dstb, n2, step * 2
                        bi2 ^= 1
                    if kn == "ps":
                        sumv = cur
                    else:
                        cntv = cur
                o0 = 8 - w_ // 2
                i = ntmp()
                i2 = ntmp()
                for tt in range(P["NT"]):
                    if L <= P["TW"]:
                        spt = P["TW"] // L
                        def win(v):
                            return v[:, tt * spt:(tt + 1) * spt, o0:o0 + L]
                        uview = upad[:, tt * spt:(tt + 1) * spt, 8:8 + L]
                        t3 = lambda a: a.rearrange("p (s t) -> p s t", s=spt)
                    else:
                        def win(v):
                            return v[:, 0, o0 + tt * 512:o0 + (tt + 1) * 512]
                        uview = upad[:, 0, 8 + tt * 512:8 + (tt + 1) * 512]
                        t3 = lambda a: a
                    kb.op("act", lambda e: e.activation(out=t3(tmpf[i][:, 0:P["TW"]]), in_=win(cntv), func=AF.Ln),
                          reads=[("pc", 0), ("pc", 1)], writes=[("tmpf", i)])
                    kb.op("act", lambda e: e.activation(out=tmpf[i][:, 0:P["TW"]], in_=tmpf[i][:, 0:P["TW"]], func=AF.Exp,
                                                        scale=-1.0),
                          reads=[("tmpf", i)], writes=[("tmpf", i)])
                    kb.op("dve", lambda e: e.tensor_tensor(out=t3(tmpf[i][:, 0:P["TW"]]), in0=t3(tmpf[i][:, 0:P["TW"]]), in1=win(sumv),
                                                           op=ALU.mult),
                          reads=[("tmpf", i), ("ps", 0), ("ps", 1)], writes=[("tmpf", i)])
                    kb.op("dve", lambda e: e.tensor_tensor(out=t3(pooled[:, gi, tsl(tt)]), in0=t3(tmpf[i][:, 0:P["TW"]]),
                                                           in1=uview, op=ALU.subtract),
                          reads=[("tmpf", i), "upad"], writes=["pooled"])
                    b = nbank()
                    kb.op("pe", lambda e: e.matmul(pb[b][:, 0:P["TW"]], lhsT=pwv[:, gi, :], rhs=pooled[:, gi, tsl(tt)],
                                                   start=True, stop=True),
                          reads=[pwk, "pooled"], writes=[("pb", b)], signal=True)
                    kb.op("act", lambda e: e.activation(out=mixed[:, gi, tsl(tt)], in_=pb[b][:, 0:P["TW"]], func=AF.Identity,
                                                        scale=ppc(l, PP_PSC + gi, 1)),
                          reads=[("pb", b), "pp"], writes=["mixed"])
            merge_branch(2, w_br_pool[l], 4, lambda kc, tt: mixed[:, kc, tsl(tt)], ["mixed"], False, None)
            mgb = mgb16
            fs = FusedStats(8)
            for half in range(2):
                wv, wk = wblock(w_out[l], 0, 8, half * 512, 512)
                for tt in range(P["NT"]):
                    for mc in range(4):
                        mo = half * 4 + mc
                        b = nbank()
                        for kc in range(8):
                            kb.op("pe", lambda e: e.matmul(pb[b][:, 0:P["TW"]], lhsT=wv[:, kc, csl(mc)], rhs=mgb[:, kc, tsl(tt)],
                                                           start=(kc == 0), stop=(kc == 7)),
                                  reads=[wk, ("mgb", tt)], writes=[("pb", b)], signal=(kc == 7))
                        kb.op("act", lambda e: e.copy(out=merged[:, mo, tsl(tt)], in_=pb[b][:, 0:P["TW"]]),
                              reads=[("pb", b)], writes=[("mg", tt)])
                        fs.add(b, tt)
            fs.finish(D)
            resid_update(5, lambda mo, t_: merged[:, mo, tsl(t_)], lambda mo, t_: [("mg", t_)])
            kb.barrier()
            dump("x_mix", xres[:, :, :], ALLX)

        ppl_loaded = {"l": None}
        for ps_id in passes:
            kb.barrier()
            xstat["ok"] = False
            if len(passes) == 2:
                wcache["mode"] = "fill" if ps_id == passes[0] else "use"
                wcache["count"][0 if ps_id == passes[0] else 1] = 0
                wcache["n"] = 0
            else:
                wcache["mode"] = "off"
            if ps_id == 0:
                P.update(T=1024, TW=512, NT=2, NCH=8)
            else:
                P.update(T=256, TW=256, NT=1, NCH=2)
                kb.barrier(engs=("pe", "act", "dve", "pool", "sp"))
                xres = R_X[:, 0:2048].rearrange("p (k t) -> p k t", k=8)
                for j in range(3):
                    slots.append(R_X[:, 2048 * (j + 1):2048 * (j + 2)].bitcast(BF16))
                NSLOT = len(slots)
            kb.dma("sp", xres[:, :, 0:P["T"]], xT_in[ps_id].rearrange("(kc p) t -> p kc t", p=128), writes=ALLX,
                   skey="xin")
            if ps_id == passes[0]:
                kb.op("act", lambda e: e.activation(out=scond[:, :, :],
                                                    in_=condsb[:, :].rearrange("p (k c) -> p k c", c=2), func=AF.Silu),
                      reads=["cond"], writes=["scond"])
            if ps_id == 0:
                post = R_A[:, 0:2050]
                kb.dma("act", post, pos_d, writes=["post"], skey="pos")
                rpos = R_A[:, 0:1024]
                cpos = R_A[:, 1024:2048]
                ang = R_A[:, 3300:3300 + 1024]
                kk = R_A[:, 4500:4500 + 1024]
                om = R_A[:, 5600:5602]
                kb.op("act", lambda e: e.activation(out=om, in_=R_A[:, 2048:2050], func=AF.Exp,
                                                    scale=-float(np.log(10000.0)) / 256.0),
                      reads=["post"], writes=["om"])
                PI = float(np.pi)
                MAGIC = 12582912.0
                for kc in range(8):
                    posv = rpos if kc < 4 else cpos
                    shift = 0.0 if (kc % 4) < 2 else PI / 2
                    kb.op("dve", lambda e: e.tensor_scalar(out=ang, in0=posv, scalar1=om[:, kc % 2:kc % 2 + 1],
                                                           scalar2=shift, op0=ALU.mult, op1=ALU.add),
                          reads=["post", "om"], writes=["ang"])
                    kb.op("dve", lambda e: e.tensor_scalar(out=kk, in0=ang, scalar1=1.0 / (2 * PI), scalar2=MAGIC,
                                                           op0=ALU.mult, op1=ALU.add),
                          reads=["ang"], writes=["kk"])
                    kb.op("dve", lambda e: e.tensor_scalar(out=kk, in0=kk, scalar1=-MAGIC, scalar2=None, op0=ALU.add),
                          reads=["kk"], writes=["kk"])
                    kb.op("dve", lambda e: e.scalar_tensor_tensor(out=ang, in0=kk, scalar=-2 * PI, in1=ang,
                                                                  op0=ALU.mult, op1=ALU.add),
                          reads=["kk", "ang"], writes=["ang"])
                    kb.op("act", lambda e: e.activation(out=ang, in_=ang, func=AF.Sin), reads=["ang"], writes=["ang"])
                    kb.op("dve", lambda e: e.scalar_tensor_tensor(out=xres[:, kc, :], in0=ang, scalar=fcol,
                                                                  in1=xres[:, kc, :], op0=ALU.mult, op1=ALU.add),
                          reads=["ang", "flag"] + ALLX, writes=ALLX)
                kb.barrier()
            dump(f"x0_{ps_id}", xres[:, :, :], ALLX)
            for l in range(n_layers):
                if ppl_loaded["l"] != (ps_id, l):
                    kb.dma("sp", ppt[l % 2][:], pp_d[:, l * PPL:(l + 1) * PPL], writes=["pp"], skey=("pp", l % 2))
                    ppl_loaded["l"] = (ps_id, l)
                if ps_id == passes[0]:
                    if l == 0:
                        bg["gen"] = mod_steps(0)
                        bg_drain()
                    if l + 1 < n_layers:
                        kb.dma("sp", ppt[(l + 1) % 2][:], pp_d[:, (l + 1) * PPL:(l + 2) * PPL], writes=["pp"],
                               skey=("pp", (l + 1) % 2))
                        ppl_loaded["l"] = (ps_id, l + 1)
                        bg["gen"] = mod_steps(l + 1)
                cur["mc"] = mcall[:, (ps_id * DEPTH + l) * 72:(ps_id * DEPTH + l + 1) * 72]
                wcache["l"] = l
                wcache["n"] = 0
                ffn(l, 0)
                mixer(l, ps_id)
                ffn(l, 1)
            kb.dma("sp", yT_out[ps_id].rearrange("(kc p) t -> p kc t", p=128), xres[:, :, 0:P["T"]], reads=ALLX,
                   skey="yout")
        kb.finish("sp")
    return nc


def _pack_inputs(inp):
    f = np.float32
    ar = lambda k: np.asarray(inp[k], dtype=f)
    pp = np.zeros((128, DEPTH, PPL), f)
    fm = lambda v: v.reshape(-1, 128).T
    for l in range(DEPTH):
        for i in range(6):
            pp[:, l, PP_NG + i * 8:PP_NG + i * 8 + 8] = fm(ar("norm_g")[l, i])
        pp[:, l, PP_BMOD:PP_BMOD + 72] = fm(ar("b_mod")[l])
        scw = ar("ssd_conv_w")[l]
        pp[:, l, PP_SCW:PP_SCW + 80] = scw.reshape(5, 16, 128).transpose(2, 1, 0).reshape(128, 80)
        pp[:, l, PP_SCB:PP_SCB + 16] = fm(ar("ssd_conv_b")[l])
        pp[:, l, PP_SNG:PP_SNG + 8] = fm(ar("ssd_norm_g")[l])
        ccw = ar("conf_conv_w")[l]
        pp[:, l, PP_CCW:PP_CCW + 124] = ccw.reshape(31, 4, 128).transpose(2, 1, 0).reshape(128, 124)
        pp[:, l, PP_CCB:PP_CCB + 4] = fm(ar("conf_conv_b")[l])
        pp[:, l, PP_CLG:PP_CLG + 4] = fm(ar("conf_ln_g")[l])
        pp[:, l, PP_CLB:PP_CLB + 4] = fm(ar("conf_ln_b")[l])
        pp[:, l, PP_PSC:PP_PSC + 4] = fm(ar("pool_scale")[l])
        pp[:, l, PP_ALOG:PP_ALOG + 32] = ar("ssd_a_log")[l].reshape(1, 32)
        pp[:, l, PP_DTB:PP_DTB + 32] = ar("ssd_dt_bias")[l].reshape(1, 32)
        pp[:, l, PP_DD:PP_DD + 16] = ar("ssd_d")[l].reshape(1, 16)
    pp = np.ascontiguousarray(pp.reshape(128, DEPTH * PPL))
    k = np.arange(128)
    cc = np.zeros((128, CC_N), f)
    cc[:, CC_U:CC_U + 128] = (k[:, None] <= k[None, :])
    cc[:, CC_LO:CC_LO + 128] = (k[:, None] >= k[None, :])
    cc[:, CC_SU:CC_SU + 128] = (k[:, None] < k[None, :])
    cc[:, CC_SL:CC_SL + 128] = (k[:, None] > k[None, :])
    cc[:, CC_ID:CC_ID + 128] = np.eye(128)
    pos = np.zeros((128, 2050), f)
    tt = np.arange(1024)
    pos[:, 0:1024] = (tt // 64)[None, :]
    pos[:, 1024:2048] = (tt % 64)[None, :]
    pos[:, 2048] = k
    pos[:, 2049] = k + 128
    return pp, cc, pos


def _core_plan(i):
    if i < 2:
        return i, None, 30 + i
    base = 5 * (i - 2)
    return None, list(range(base, base + 4)), base + 4


def _core_inputs(inp, i, shared):
    f = np.float32
    xp, xs, st = inp["x_prompt"], inp["x_sample"], inp["state_ssd"]
    c, c_ctx = inp["c"], inp["c_ctx"]
    b, pa, pb_ = _core_plan(i)
    m = dict(shared)
    if b is not None:
        m["xT_a"] = np.ascontiguousarray(xs[b].T, dtype=f)
        m["st_in"] = np.ascontiguousarray(st[b].reshape(DEPTH, 2, 1024, 128).transpose(0, 1, 3, 2), dtype=f)
        conda = c[b]
        flag = 1.0
    else:
        m["xT_a"] = np.ascontiguousarray(xp[pa[0]:pa[0] + 4].reshape(T, D).T, dtype=f)
        m["st_in"] = np.zeros((DEPTH, 2, 128, 1024), f)
        conda = c_ctx
        flag = 0.0
    m["xT_b"] = np.ascontiguousarray(xp[pb_].T, dtype=f)
    cond = np.stack([conda, c_ctx], axis=-1)
    m["condT"] = np.ascontiguousarray(cond.reshape(8, 128, 2).transpose(1, 0, 2).reshape(128, 16), dtype=f)
    m["flag"] = np.full((128, 2), flag, f)
    return m


def kernel(**inp):
    f = np.float32
    inp = {k: np.asarray(v, dtype=f) for k, v in inp.items()}
    pp, cc, pos = _pack_inputs(inp)
    shared = {k: np.ascontiguousarray(inp[k]) for k in ("w_mod", "w_ffn_in", "w_ffn_out", "w_in", "w_br_ssd", "w_br_conf",
                                                        "w_br_pool", "pool_w", "w_out")}
    shared.update(pp=pp, cc=cc, pos=pos)
    in_maps = [_core_inputs(inp, i, shared) for i in range(NCORES)]
    nc = build_program()
    res = run_bass_kernel_spmd(nc, in_maps, core_ids=list(range(NCORES)))
    R = res.results
    y_p = np.zeros((32, 256, D), f)
    y_s = np.zeros((2, 1024, D), f)
    ns = np.zeros((32, DEPTH, 2, 16, 64, 128), f)
    for i in range(NCORES):
        b, pa, pb_ = _core_plan(i)
        so = R[i]["st_out"].transpose(0, 1, 2, 4, 3).reshape(5, DEPTH, 2, 16, 64, 128)
        if b is not None:
            y_s[b] = R[i]["yT_a"].T
        else:
            y_p[pa[0]:pa[0] + 4] = R[i]["yT_a"].T.reshape(4, 256, D)
            ns[pa[0]:pa[0] + 4] = so[0:4]
        y_p[pb_] = R[i]["yT_b"].T
        ns[pb_] = so[4]
    return (y_p, y_s, ns)
```

```python
import numpy as np
from contextlib import ExitStack
import concourse.bass as bass
import concourse.mybir as mybir
from concourse.bass_utils import run_bass_kernel_spmd

F32 = mybir.dt.float32
BF16 = mybir.dt.bfloat16
ALU = mybir.AluOpType
AF = mybir.ActivationFunctionType

D = 1024
DEPTH = 4
T = 1024
NT = 2
NCH = 8
DFF = 2816
EPS = 1e-6
C_Z, C_XBC, C_DT, C_CONF, C_POOL, C_GATE = 0, 1024, 3072, 3104, 4128, 4640
SEM_LIMIT = 8000
NCORES = 8

PP_NG, PP_BMOD, PP_SCW, PP_SCB, PP_SNG = 0, 48, 120, 200, 216
PP_CCW, PP_CCB, PP_CLG, PP_CLB, PP_PSC = 224, 348, 352, 356, 360
PP_ALOG, PP_DTB, PP_DD = 364, 396, 428
PPL = 444
CC_U, CC_LO, CC_SU, CC_SL, CC_ID, CC_N = 0, 128, 256, 384, 512, 640


class _Eng:
    def __init__(self, kb, name, handle):
        self.kb = kb
        self.name = name
        self.h = handle
        self.nsem = 0
        self.sem = kb.new_sem(f"{name}_s0")
        self.cnt = 0
        self.seen = {}
        self.in_chain = False

    def rotate(self):
        if self.cnt >= SEM_LIMIT:
            self.nsem += 1
            self.sem = self.kb.new_sem(f"{self.name}_s{self.nsem}")
            self.cnt = 0


class KB:
    def __init__(self, nc, stack):
        self.nc = nc
        self.stack = stack
        self.bufs = {}
        self.dsem = {}
        self.eng = {
            "pe": _Eng(self, "pe", nc.tensor),
            "act": _Eng(self, "act", nc.scalar),
            "dve": _Eng(self, "dve", nc.vector),
            "pool": _Eng(self, "pool", nc.gpsimd),
            "sp": _Eng(self, "sp", nc.sync),
        }

    def new_sem(self, name):
        return self.stack.enter_context(self.nc.semaphore(name))

    def sb(self, name, shape, dt):
        return self.stack.enter_context(self.nc.sbuf_tensor("sb_" + name, shape, dt))

    def ps(self, name, shape, dt):
        return self.stack.enter_context(self.nc.psum_tensor("ps_" + name, shape, dt))

    def _need(self, reads, writes):
        need = {}

        def req(d):
            for s, v in d.items():
                if need.get(s, 0) < v:
                    need[s] = v

        for k in reads:
            st = self.bufs.get(k)
            if st:
                req(st["w"])
        for k in writes:
            st = self.bufs.get(k)
            if st:
                req(st["w"])
                req(st["r"])
        return need

    def _waits(self, E, need):
        for s, v in need.items():
            if E.name == "pe" and s is E.sem:
                continue
            if E.seen.get(s, 0) >= v:
                continue
            E.h.wait_ge(s, v)
            E.seen[s] = v

    def _record(self, reads, writes, sem, val):
        for k in reads:
            st = self.bufs.setdefault(k, {"w": {}, "r": {}})
            if st["r"].get(sem, 0) < val:
                st["r"][sem] = val
        for k in writes:
            st = self.bufs.setdefault(k, {"w": {}, "r": {}})
            if st["w"].get(sem, 0) < val:
                st["w"][sem] = val

    def op(self, en, fn, reads=(), writes=(), signal=True):
        E = self.eng[en]
        if signal and not E.in_chain:
            E.rotate()
        E.in_chain = not signal
        self._waits(E, self._need(reads, writes))
        ins = fn(E.h)
        sem, val = E.sem, E.cnt + 1
        if signal:
            ins.then_inc(E.sem, 1)
            E.cnt += 1
        self._record(reads, writes, sem, val)
        return ins

    def dma(self, q, out, in_, reads=(), writes=(), skey=None):
        E = self.eng[q]
        self._waits(E, self._need(reads, writes))
        if skey not in self.dsem:
            self.dsem[skey] = [self.new_sem(f"d_{len(self.dsem)}"), 0]
        ds = self.dsem[skey]
        ins = E.h.dma_start(out=out, in_=in_)
        ds[1] += 16
        ins.then_inc(ds[0], 16)
        self._record(reads, writes, ds[0], ds[1])
        return ins

    def barrier(self, engs=("pe", "act", "dve")):
        for a in engs:
            A = self.eng[a]
            need = {}
            for b in engs:
                if b == a:
                    continue
                B = self.eng[b]
                if B.cnt > 0:
                    need[B.sem] = B.cnt
            for k, (ds, dv) in self.dsem.items():
                if not (isinstance(k, tuple) and k[0] == "w") and dv > 0:
                    need[ds] = dv
            self._waits(A, need)

    def finish(self, en="sp"):
        E = self.eng[en]
        need = {}
        for s, v in self.dsem.values():
            need[s] = v
        for b in self.eng.values():
            if b is not E and b.cnt > 0:
                need[b.sem] = b.cnt
        self._waits(E, need)


def build_program(n_layers=DEPTH, passes=(0, 1), dbg=None):
    nc = bass.Bass("TRN2", target_bir_lowering=False)
    dr = {}

    def din(name, shape):
        dr[name] = nc.dram_tensor(name, list(shape), F32, kind="ExternalInput").ap()
        return dr[name]

    def dout(name, shape):
        dr[name] = nc.dram_tensor(name, list(shape), F32, kind="ExternalOutput").ap()
        return dr[name]

    TB = 256
    xT_in = [din("xT_a", (D, T)), din("xT_b", (D, TB))]
    flag_d = din("flag", (128, 2))
    st_in = din("st_in", (DEPTH, 2, 128, 1024))
    condT = din("condT", (128, 16))
    pp_d = din("pp", (128, DEPTH * PPL))
    cc_d = din("cc", (128, CC_N))
    pos_d = din("pos", (128, 2050))
    w_mod = din("w_mod", (DEPTH, D, 9 * D))
    w_ffn_in = din("w_ffn_in", (DEPTH, 2, D, 2 * DFF))
    w_ffn_out = din("w_ffn_out", (DEPTH, 2, DFF, D))
    w_in = din("w_in", (DEPTH, D, 7712))
    w_br_ssd = din("w_br_ssd", (DEPTH, 1024, D))
    w_br_conf = din("w_br_conf", (DEPTH, 512, D))
    w_br_pool = din("w_br_pool", (DEPTH, 512, D))
    pool_w = din("pool_w", (DEPTH, 4, 128, 128))
    w_out = din("w_out", (DEPTH, D, D))
    yT_out = [dout("yT_a", (D, T)), dout("yT_b", (D, TB))]
    st_out = dout("st_out", (5, DEPTH, 2, 128, 1024))
    wscrs = [nc.dram_tensor(f"wscr{l}", [70, 128, 4096], BF16, kind="Internal").ap() for l in range(DEPTH)]
    dbg_out = {}
    if dbg:
        for k, shp in dbg.items():
            dbg_out[k] = dout("dbg_" + k, shp)

    with ExitStack() as stack:
        kb = KB(nc, stack)
        R_X = kb.sb("R_X", [128, 8192], F32)
        R_HY = kb.sb("R_HY", [128, 8192], F32)
        R_A = kb.sb("R_A", [128, 11264], F32)
        R_MX = kb.sb("R_MX", [128, 12288], F32)
        NSLOT = 2
        slots = [kb.sb(f"wslot{i}", [128, 4096], BF16) for i in range(NSLOT)]
        ppt = [kb.sb(f"ppt{i}", [128, PPL], F32) for i in range(2)]
        mcall = kb.sb("mcall", [128, 2 * DEPTH * 72], F32)
        rsegs = [kb.sb(f"rseg{i}", [128, 4, 128], F32) for i in range(2)]
        dgb = [kb.sb(f"dgb{i}", [128, 128], BF16) for i in range(4)]
        cc = kb.sb("cc", [128, CC_N], F32)
        cb = kb.sb("cb", [128, 512], BF16)
        onesf = kb.sb("onesf", [128, 128], F32)
        condsb = kb.sb("condsb", [128, 16], F32)
        flagsb = kb.sb("flagsb", [128, 2], F32)
        scond = kb.sb("scond", [128, 8, 2], BF16)
        modT = kb.sb("modT", [128, 72], F32)
        rstd = [kb.sb(f"rstd{i}", [128, 512], F32) for i in range(2)]
        tmpf = [kb.sb(f"tmpf{i}", [128, 512], F32) for i in range(3)]
        sqb = [kb.sb(f"sqb{i}", [128, 512], BF16) for i in range(2)]
        sst = kb.sb("sst", [128, 512], F32)
        decb = [kb.sb(f"decb{i}", [128, 4, 128], BF16) for i in range(2)]
        mtb = [kb.sb(f"mtb{i}", [128, 4, 128], BF16) for i in range(2)]
        lay = kb.sb("lay", [128, 128], F32)
        dtt = kb.sb("dtt", [128, NCH, 32], F32)
        adt = kb.sb("adt", [128, NCH, 32], F32)
        pb = [kb.ps(f"pb{i}", [128, 512], F32) for i in range(7)]
        pbt = kb.ps("pbt", [128, 1024], BF16)

        xres = R_X[:, :].rearrange("p (k t) -> p k t", k=8)
        hbuf = R_HY[:, 0:4096].bitcast(BF16).rearrange("p (k t) -> p k t", k=8)
        zs = R_HY[:, 4096:8192].bitcast(BF16).rearrange("p (c f) -> p c f", c=8)
        ybuf = R_HY[:, :].rearrange("p (k t) -> p k t", k=8)
        mgb16 = R_HY[:, 4096:8192].bitcast(BF16).rearrange("p (k t) -> p k t", k=8)
        abuf = R_A[:, :].bitcast(BF16).rearrange("p (k t) -> p k t", k=22)
        merged = R_A[:, 0:8192].rearrange("p (k t) -> p k t", k=8)
        Sst = [R_A[:, 8192:9216], R_A[:, 9216:10240]]
        Sbb = R_A[:, 10240:10752].bitcast(BF16)
        GMv = [R_A[:, 10752:11008].bitcast(BF16).rearrange("p (g l) -> p g l", g=4),
               R_A[:, 11008:11264].bitcast(BF16).rearrange("p (g l) -> p g l", g=4)]
        x_tok = R_HY[:, 0:512].bitcast(BF16)
        B_toks = [R_HY[:, 512:768].bitcast(BF16), R_HY[:, 3840:4096].bitcast(BF16)]
        xdt = [R_HY[:, 768:1280].bitcast(BF16), R_HY[:, 1280:1792].bitcast(BF16)]
        xdte = R_HY[:, 1792:2304].bitcast(BF16)
        ysb = R_HY[:, 2304:3328]
        ttb = R_HY[:, 3328:3840].bitcast(BF16)
        xs_fm = R_MX[:, 0:4096].bitcast(BF16).rearrange("p (k t) -> p k t", k=8)
        B_fm = R_MX[:, 4096:6144].bitcast(BF16).rearrange("p (k t) -> p k t", k=4)
        C_fm = R_MX[:, 6144:8192].bitcast(BF16).rearrange("p (k t) -> p k t", k=4)
        Sf_all = R_MX[:, 8192:12288].bitcast(BF16).rearrange("p (c f) -> p c f", c=8)

        U_f, Lo_f = cc[:, CC_U:CC_U + 128], cc[:, CC_LO:CC_LO + 128]
        SU_f, SL_f = cc[:, CC_SU:CC_SU + 128], cc[:, CC_SL:CC_SL + 128]
        id_f = cc[:, CC_ID:CC_ID + 128]
        U_b, Lo_b, id_b, ones_b = cb[:, 0:128], cb[:, 128:256], cb[:, 256:384], cb[:, 384:512]

        ALLX = [("x", tt) for tt in range(2)]
        ALLH = [("h", tt) for tt in range(2)]
        ALLZS = [("zs", c) for c in range(8)]

        P = {"T": 1024, "TW": 512, "NT": 2, "NCH": 8}

        def tsl(tt):
            return slice(tt * P["TW"], (tt + 1) * P["TW"])

        def fsl(hf):
            return slice(hf * 512, (hf + 1) * 512)

        def csl(c):
            return slice(c * 128, (c + 1) * 128)

        kb.dma("sp", cc[:], cc_d, writes=["cc"], skey="c1")
        kb.dma("sp", condsb[:], condT, writes=["cond"], skey="c2")
        kb.dma("sp", flagsb[:], flag_d, writes=["flag"], skey="c3")
        fcol = flagsb[:, 0:1]
        kb.op("dve", lambda e: e.tensor_copy(out=cb[:, 0:256], in_=cc[:, 0:256]), reads=["cc"], writes=["cb"])
        kb.op("dve", lambda e: e.tensor_copy(out=cb[:, 256:384], in_=id_f), reads=["cc"], writes=["cb"])
        kb.op("dve", lambda e: e.memset(cb[:, 384:512], 1.0), writes=["cb"])
        kb.op("dve", lambda e: e.memset(onesf[:], 1.0), writes=["onesf"])

        wctr = [0]

        wcache = {"n": 0, "mode": "fill", "count": [0, 0]}

        def wload(parts, cache=True):
            s = wctr[0] % NSLOT
            wctr[0] += 1
            key = ("w", s)
            if cache and wcache["mode"] == "use":
                n = wcache["n"]
                wcache["n"] += 1
                ly = wcache["l"]
                kb.dma("sp", slots[s][:, 0:4096], wscrs[ly][n], reads=[("wscr", ly, n)], writes=[key], skey=key)
                return slots[s], key
            for dst, src in parts:
                kb.dma("pool", dst(slots[s]), src, writes=[key], skey=key)
            if cache and wcache["mode"] == "fill":
                n = wcache["n"]
                wcache["n"] += 1
                ly = wcache["l"]
                kb.dma("sp", wscrs[ly][n], slots[s][:, 0:4096], reads=[key], writes=[("wscr", ly, n)], skey=("wout", s))
            return slots[s], key

        def wblock(wmat, r0, kc_n, c0, ncols, cache=True):
            src = wmat[r0:r0 + kc_n * 128, c0:c0 + ncols].rearrange("(kc p) m -> p kc m", p=128)
            n = kc_n * ncols
            sl, key = wload([(lambda s: s[:, 0:n].rearrange("p (kc m) -> p kc m", kc=kc_n), src)], cache=cache)
            return sl[:, 0:n].rearrange("p (kc m) -> p kc m", kc=kc_n), key

        bankctr = [0]

        def nbank():
            b = bankctr[0] % 5
            bankctr[0] += 1
            return b

        tctr = [0]

        def ntmp():
            i = tctr[0] % 3
            tctr[0] += 1
            return i

        def dump(name, ap, keys):
            if name in dbg_out:
                kb.dma("sp", dbg_out[name], ap, reads=keys, skey=("dbg", name))

        def ppc(l, off, n=1):
            return ppt[l % 2][:, off: off + n]

        cur = {"mc": None}

        def mcf(c0, n=1):
            return cur["mc"][:, c0:c0 + n]

        def rms_stats(src_fn, src_keys_fn, KC, dim, tt):
            for kc in range(KC):
                q = kc % 2
                kb.op("act", lambda e: e.activation(out=sqb[q][:, 0:P["TW"]], in_=src_fn(kc, tt), func=AF.Square),
                      reads=src_keys_fn(kc, tt), writes=[("sqb", q)])
                kb.op("pe", lambda e: e.matmul(pb[6][:, 0:P["TW"]], lhsT=ones_b, rhs=sqb[q][:, 0:P["TW"]], start=(kc == 0),
                                               stop=(kc == KC - 1)),
                      reads=[("sqb", q), "cb"], writes=[("pb", 6)], signal=True)
            kb.op("act", lambda e: e.activation(out=rstd[tt][:, 0:P["TW"]], in_=pb[6][:, 0:P["TW"]], func=AF.Ln, bias=EPS,
                                                scale=1.0 / dim),
                  reads=[("pb", 6)], writes=[("rstd", tt)])
            kb.op("act", lambda e: e.activation(out=rstd[tt][:, 0:P["TW"]], in_=rstd[tt][:, 0:P["TW"]], func=AF.Exp, scale=-0.5),
                  reads=[("rstd", tt)], writes=[("rstd", tt)])

        def fin_rstd(tt, bank, dim):
            kb.op("act", lambda e: e.activation(out=rstd[tt][:, 0:P["TW"]], in_=pb[bank][:, 0:P["TW"]], func=AF.Ln, bias=EPS,
                                                scale=1.0 / dim),
                  reads=[("pb", bank)], writes=[("rstd", tt)])
            kb.op("act", lambda e: e.activation(out=rstd[tt][:, 0:P["TW"]], in_=rstd[tt][:, 0:P["TW"]], func=AF.Exp, scale=-0.5),
                  reads=[("rstd", tt)], writes=[("rstd", tt)])

        SBANK = (6, 5)
        sqc = [0]

        class FusedStats:
            def __init__(self, nsteps):
                self.pend = None
                self.n = nsteps
                self.cnt = [0, 0]

            def _emit(self, p):
                q, tt = p
                i = self.cnt[tt]
                self.cnt[tt] += 1
                kb.op("pe", lambda e: e.matmul(pb[SBANK[tt]][:, 0:P["TW"]], lhsT=ones_b, rhs=sqb[q][:, 0:P["TW"]], start=(i == 0),
                                               stop=(i == self.n - 1)),
                      reads=[("sqb", q), "cb"], writes=[("pb", SBANK[tt])], signal=True)

            def add(self, b, tt):
                self.add_src(pb[b][:, 0:P["TW"]], [("pb", b)], tt)

            def add_src(self, src, keys, tt):
                q = sqc[0] % 2
                sqc[0] += 1
                kb.op("act", lambda e: e.activation(out=sqb[q][:, 0:P["TW"]], in_=src, func=AF.Square),
                      reads=keys, writes=[("sqb", q)])
                if self.pend:
                    self._emit(self.pend)
                self.pend = (q, tt)

            def finish(self, dim):
                self._emit(self.pend)
                for tt in range(P["NT"]):
                    fin_rstd(tt, SBANK[tt], dim)

        xstat = {"ok": False}

        def make_h(ai, stash=None, use_stash=None):
            have = xstat["ok"] or (use_stash is not None)
            xstat["ok"] = False
            for tt in range(P["NT"]):
                if not have:
                    rms_stats(lambda kc, t_: xres[:, kc, tsl(t_)], lambda kc, t_: [("x", t_)], 8, D, tt)
                rs_ap, rs_key = rstd[tt][:, 0:P["TW"]], ("rstd", tt)
                if use_stash is not None:
                    rs_ap, rs_key = use_stash[tt][:, 0:P["TW"]], ("rstash", tt)
                if stash is not None:
                    kb.op("act", lambda e: e.copy(out=stash[tt][:, 0:P["TW"]], in_=rs_ap), reads=[rs_key],
                          writes=[("rstash", tt)])
                for kc in range(8):
                    i = ntmp()
                    kb.op("dve", lambda e: e.tensor_tensor(out=tmpf[i][:, 0:P["TW"]], in0=xres[:, kc, tsl(tt)],
                                                           in1=rs_ap, op=ALU.mult),
                          reads=[("x", tt), rs_key], writes=[("tmpf", i)])
                    kb.op("act", lambda e: e.activation(out=hbuf[:, kc, tsl(tt)], in_=tmpf[i][:, 0:P["TW"]],
                                                        func=AF.Identity,
                                                        bias=mcf((ai + 1) * 8 + kc),
                                                        scale=mcf(ai * 8 + kc)),
                          reads=[("tmpf", i), "mcoef"], writes=[("h", tt)])

        def resid_update(ci, src_fn, src_keys_fn):
            fsx = FusedStats(8)
            for tt in range(P["NT"]):
                for mo in range(8):
                    i = ntmp()
                    kb.op("dve", lambda e: e.tensor_tensor(out=tmpf[i][:, 0:P["TW"]], in0=src_fn(mo, tt), in1=rstd[tt][:, 0:P["TW"]],
                                                           op=ALU.mult),
                          reads=src_keys_fn(mo, tt) + [("rstd", tt)], writes=[("tmpf", i)])
                    kb.op("dve", lambda e: e.scalar_tensor_tensor(out=xres[:, mo, tsl(tt)], in0=tmpf[i][:, 0:P["TW"]],
                                                                  scalar=mcf(ci * 8 + mo),
                                                                  in1=xres[:, mo, tsl(tt)], op0=ALU.mult,
                                                                  op1=ALU.add),
                          reads=[("tmpf", i), ("x", tt), "mcoef"], writes=[("x", tt)])
                    fsx.add_src(xres[:, mo, tsl(tt)], [("x", tt)], tt)
            fsx.finish(D)
            xstat["ok"] = True

        def mod_steps(l):
            for blk in range(18):
                wv, wk = wblock(w_mod[l], 0, 8, blk * 512, 512, cache=False)
                for mc in range(4):
                    j = blk * 4 + mc
                    for kc in range(8):
                        kb.op("pe", lambda e: e.matmul(pb[6][:, 2 * j:2 * j + 2], lhsT=wv[:, kc, csl(mc)],
                                                       rhs=scond[:, kc, :], start=(kc == 0), stop=(kc == 7)),
                              reads=[wk, "scond"], writes=[("pb", 6)], signal=(kc == 7))
                yield
            pv = pb[6][:, 0:144].rearrange("p (j c) -> p j c", c=2)
            for ci in range(2):
                mco = mcall[:, (ci * DEPTH + l) * 72:(ci * DEPTH + l + 1) * 72]
                kb.op("dve", lambda e: e.tensor_tensor(out=modT[:, :], in0=pv[:, :, ci], in1=ppc(l, PP_BMOD, 72),
                                                       op=ALU.add),
                      reads=[("pb", 6), "pp"], writes=["modT"])
                for s_ in range(3):
                    sh, sc, g = (modT[:, (3 * s_ + q) * 8:(3 * s_ + q) * 8 + 8] for q in range(3))
                    nga = ppc(l, PP_NG + (2 * s_) * 8, 8)
                    ngb = ppc(l, PP_NG + (2 * s_ + 1) * 8, 8)
                    kb.op("dve", lambda e: e.scalar_tensor_tensor(out=mco[:, (3 * s_) * 8:(3 * s_) * 8 + 8], in0=sc,
                                                                  scalar=1.0, in1=nga, op0=ALU.add, op1=ALU.mult),
                          reads=["modT", "pp"], writes=["mcoef"])
                    kb.op("dve", lambda e: e.tensor_copy(out=mco[:, (3 * s_ + 1) * 8:(3 * s_ + 1) * 8 + 8], in_=sh),
                          reads=["modT"], writes=["mcoef"])
                    kb.op("dve", lambda e: e.scalar_tensor_tensor(out=mco[:, (3 * s_ + 2) * 8:(3 * s_ + 2) * 8 + 8],
                                                                  in0=g, scalar=(1.0 if s_ == 1 else 0.5), in1=ngb,
                                                                  op0=ALU.mult, op1=ALU.mult),
                          reads=["modT", "pp"], writes=["mcoef"])
            yield

        bg = {"gen": None}

        def bg_step(n=1):
            for _ in range(n):
                if bg["gen"] is not None:
                    try:
                        next(bg["gen"])
                    except StopIteration:
                        bg["gen"] = None

        def bg_drain():
            while bg["gen"] is not None:
                bg_step()

        def ffn(l, which):
            s3 = 0 if which == 0 else 2
            make_h(3 * s3)
            dump(f"h{which}", hbuf[:, :, :], ALLH)
            wi = w_ffn_in[l, which]
            for j in range(11):
                srcg = wi[:, j * 256:(j + 1) * 256].rearrange("(kc p) m -> p kc m", p=128)
                srcu = wi[:, DFF + j * 256:DFF + (j + 1) * 256].rearrange("(kc p) m -> p kc m", p=128)
                sl, wk = wload([
                    (lambda s: s[:, 0:2048].rearrange("p (kc m) -> p kc m", kc=8), srcg),
                    (lambda s: s[:, 2048:4096].rearrange("p (kc m) -> p kc m", kc=8), srcu)])
                wg = sl[:, 0:2048].rearrange("p (kc m) -> p kc m", kc=8)
                wu = sl[:, 2048:4096].rearrange("p (kc m) -> p kc m", kc=8)
                for tt in range(P["NT"]):
                    for mc in range(2):
                        bg, bu = nbank(), nbank()
                        for kc in range(8):
                            kb.op("pe", lambda e: e.matmul(pb[bg][:, 0:P["TW"]], lhsT=wg[:, kc, csl(mc)],
                                                           rhs=hbuf[:, kc, tsl(tt)], start=(kc == 0), stop=(kc == 7)),
                                  reads=[wk, ("h", tt)], writes=[("pb", bg)], signal=(kc == 7))
                        for kc in range(8):
                            kb.op("pe", lambda e: e.matmul(pb[bu][:, 0:P["TW"]], lhsT=wu[:, kc, csl(mc)],
                                                           rhs=hbuf[:, kc, tsl(tt)], start=(kc == 0), stop=(kc == 7)),
                                  reads=[wk, ("h", tt)], writes=[("pb", bu)], signal=(kc == 7))
                        i = ntmp()
                        kb.op("act", lambda e: e.activation(out=tmpf[i][:, 0:P["TW"]], in_=pb[bg][:, 0:P["TW"]], func=AF.Silu),
                              reads=[("pb", bg)], writes=[("tmpf", i)])
                        kb.op("dve", lambda e: e.tensor_tensor(out=abuf[:, 2 * j + mc, tsl(tt)], in0=tmpf[i][:, 0:P["TW"]],
                                                               in1=pb[bu][:, 0:P["TW"]], op=ALU.mult),
                              reads=[("tmpf", i), ("pb", bu)], writes=[("a", tt)])
            wo = w_ffn_out[l, which]
            fs = FusedStats(8)
            for mo in range(8):
                srcw = wo[:, mo * 128:(mo + 1) * 128].rearrange("(kc p) m -> p kc m", p=128)
                sl, wk = wload([(lambda s_: s_[:, 0:2816].rearrange("p (kc m) -> p kc m", kc=22), srcw)])
                wv = sl[:, 0:2816].rearrange("p (kc m) -> p kc m", kc=22)
                for tt in range(P["NT"]):
                    b = nbank()
                    for kc in range(22):
                        kb.op("pe", lambda e: e.matmul(pb[b][:, 0:P["TW"]], lhsT=wv[:, kc, :], rhs=abuf[:, kc, tsl(tt)],
                                                       start=(kc == 0), stop=(kc == 21)),
                              reads=[wk, ("a", tt)], writes=[("pb", b)], signal=(kc == 21))
                    kb.op("act", lambda e: e.copy(out=ybuf[:, mo, tsl(tt)], in_=pb[b][:, 0:P["TW"]]),
                          reads=[("pb", b)], writes=[("y", tt)])
                    fs.add(b, tt)
            fs.finish(D)
            resid_update(3 * s3 + 2, lambda mo, t_: ybuf[:, mo, tsl(t_)], lambda mo, t_: [("y", t_)])
            dump(f"x_ffn{which}", xres[:, :, :], ALLX)

        def dwconv(pad3, acc3, wcols, K, keys_pad, keys_acc, center_done=False):
            L = acc3.shape[2]
            if not center_done:
                kb.op("dve", lambda e: e.tensor_scalar(out=acc3, in0=pad3[:, :, 0:L], scalar1=wcols[:, 0:1], scalar2=None,
                                                       op0=ALU.mult),
                      reads=keys_pad + ["pp"], writes=keys_acc)
            for k in range(0 if center_done else 1, K):
                if center_done and k == K // 2:
                    continue
                kb.op("dve", lambda e: e.scalar_tensor_tensor(out=acc3, in0=pad3[:, :, k:k + L],
                                                              scalar=wcols[:, k:k + 1], in1=acc3, op0=ALU.mult,
                                                              op1=ALU.add),
                      reads=keys_pad + keys_acc + ["pp"], writes=keys_acc)

        def mixer(l, ps_id):
            nseq, L = (4, 256) if ps_id == 0 else (1, 256)
            cpl = L // 128
            wi = w_in[l]
            rstash = [R_A[:, 5000:5512], R_A[:, 5512:6024]]
            make_h(3, stash=rstash)
            dump("h_mix", hbuf[:, :, :], ALLH)
            kb.op("act", lambda e: e.activation(out=lay[:, 0:32], in_=ppc(l, PP_ALOG, 32), func=AF.Exp),
                  reads=["pp"], writes=["lay"])
            kb.op("dve", lambda e: e.tensor_scalar(out=lay[:, 0:32], in0=lay[:, 0:32], scalar1=-1.0, scalar2=None,
                                                   op0=ALU.mult), reads=["lay"], writes=["lay"])
            PW = L + 4
            padvs = [R_A[:, i * 1100:i * 1100 + nseq * PW].rearrange("p (s t) -> p s t", s=nseq) for i in range(2)]
            accfl = [R_A[:, 2300 + i * 1024:2300 + i * 1024 + P["T"]] for i in range(2)]
            kb.op("dve", lambda e: e.memset(R_A[:, 0:2200], 0.0), writes=[("cpad", 0), ("cpad", 1)])
            def xbc_tail(ch):
                dwconv(padvs[ch % 2], accfl[ch % 2].rearrange("p (s t) -> p s t", s=nseq), ppc(l, PP_SCW + ch * 5, 5), 5,
                       [("cpad", ch % 2)], [("xacc", ch % 2)], center_done=True)
                if ch < 8:
                    dstv = xs_fm[:, ch, 0:P["T"]]
                elif ch < 12:
                    dstv = B_fm[:, ch - 8, 0:P["T"]]
                else:
                    dstv = C_fm[:, ch - 12, 0:P["T"]]
                kb.op("act", lambda e: e.activation(out=dstv, in_=accfl[ch % 2], func=AF.Silu,
                                                    bias=ppc(l, PP_SCB + ch, 1), scale=1.0),
                      reads=[("xacc", ch % 2), "pp"], writes=["mx"])

            for blk in range(4):
                wv, wk = wblock(wi, 0, 8, C_XBC + blk * 512, 512)
                for mc in range(4):
                    ch = blk * 4 + mc
                    padv = padvs[ch % 2]
                    for tt in range(P["NT"]):
                        b = nbank()
                        for kc in range(8):
                            kb.op("pe", lambda e: e.matmul(pb[b][:, 0:P["TW"]], lhsT=wv[:, kc, csl(mc)], rhs=hbuf[:, kc, tsl(tt)],
                                                           start=(kc == 0), stop=(kc == 7)),
                                  reads=[wk, ("h", tt)], writes=[("pb", b)], signal=(kc == 7))
                        spt = P["TW"] // L
                        if L <= P["TW"]:
                            dst = padv[:, tt * spt:(tt + 1) * spt, 2:2 + L]
                            src = pb[b][:, 0:P["TW"]].rearrange("p (s t) -> p s t", s=spt)
                        else:
                            dst = padv[:, 0, 2 + tt * 512:2 + (tt + 1) * 512]
                            src = pb[b][:, 0:P["TW"]]
                        kb.op("act", lambda e: e.copy(out=dst, in_=src), reads=[("pb", b)], writes=[("cpad", ch % 2)])
                        kb.op("act", lambda e: e.activation(out=accfl[ch % 2][:, tsl(tt)], in_=pb[b][:, 0:P["TW"]],
                                                            func=AF.Identity, scale=ppc(l, PP_SCW + ch * 5 + 2, 1)),
                              reads=[("pb", b), "pp"], writes=[("xacc", ch % 2)])
                    if nseq > 1:
                        kb.op("dve", lambda e: e.tensor_scalar(out=padv[:, 1:nseq, 0:2], in0=padv[:, 0:nseq - 1, L:L + 2],
                                                               scalar1=fcol, scalar2=None, op0=ALU.mult),
                              reads=[("cpad", ch % 2), "flag"], writes=[("cpad", ch % 2)])
                        kb.op("dve", lambda e: e.tensor_scalar(out=padv[:, 0:nseq - 1, L + 2:L + 4], in0=padv[:, 1:nseq, 2:4],
                                                               scalar1=fcol, scalar2=None, op0=ALU.mult),
                              reads=[("cpad", ch % 2), "flag"], writes=[("cpad", ch % 2)])
                    if ch >= 1:
                        xbc_tail(ch - 1)
            xbc_tail(15)
            dump("xs_fm", xs_fm[:, :, :], ["mx"])
            dump("B_fm", B_fm[:, :, :], ["mx"])
            wv, wk = wblock(wi, 0, 8, C_DT, 32)
            for c in range(P["NCH"]):
                for kc in range(8):
                    kb.op("pe", lambda e: e.matmul(pb[6][:, c * 32:(c + 1) * 32], lhsT=hbuf[:, kc, csl(c)],
                                                   rhs=wv[:, kc, :], start=(kc == 0), stop=(kc == 7)),
                          reads=[wk, ("h", c // (P["TW"] // 128))], writes=[("pb", 6)], signal=(kc == 7))
            dt3 = dtt[:, 0:P["NCH"], :]
            kb.op("dve", lambda e: e.tensor_tensor(out=dt3, in0=pb[6][:, 0:32 * P["NCH"]].rearrange("p (c h) -> p c h", c=P["NCH"]),
                                                   in1=ppc(l, PP_DTB, 32).unsqueeze(1).to_broadcast([128, P["NCH"], 32]),
                                                   op=ALU.add),
                  reads=[("pb", 6), "pp"], writes=["dtt"])
            kb.op("act", lambda e: e.activation(out=dt3, in_=dt3, func=AF.Exp), reads=["dtt"], writes=["dtt"])
            kb.op("act", lambda e: e.activation(out=dt3, in_=dt3, func=AF.Ln, bias=1.0, scale=1.0),
                  reads=["dtt"], writes=["dtt"])
            kb.op("dve", lambda e: e.tensor_tensor(out=adt[:, 0:P["NCH"], :], in0=dt3,
                                                   in1=lay[:, 0:32].unsqueeze(1).to_broadcast([128, P["NCH"], 32]),
                                                   op=ALU.mult),
                  reads=["dtt", "lay"], writes=["adt"])
            dump("dtt", dtt[:, :, :], ["dtt"])
            for half in range(2):
                wv, wk = wblock(wi, 0, 8, C_Z + half * 512, 512)
                for c in range(P["NCH"]):
                    b = nbank()
                    for kc in range(8):
                        kb.op("pe", lambda e: e.matmul(pb[b][:, 0:512], lhsT=hbuf[:, kc, csl(c)], rhs=wv[:, kc, :],
                                                       start=(kc == 0), stop=(kc == 7)),
                              reads=[wk, ("h", c // (P["TW"] // 128))], writes=[("pb", b)], signal=(kc == 7))
                    kb.op("act", lambda e: e.activation(out=zs[:, c, fsl(half)], in_=pb[b][:, 0:512], func=AF.Silu),
                          reads=[("pb", b)], writes=[("zs", c)])
            Dbc = ppc(l, PP_DD, 16)

            par = {"p": 0}

            def sstv(pr):
                return sst[:, pr * 256:pr * 256 + 96]

            def prep(c, dirs, pr):
                for k in range(8):
                    kb.op("pe", lambda e: e.transpose(pbt[:, k * 128:(k + 1) * 128], xs_fm[:, k, csl(c)], id_b),
                          reads=["mx", "cb"], writes=["pbt"], signal=(k == 7))
                kb.op("act", lambda e: e.copy(out=x_tok, in_=pbt[:, :]), reads=["pbt"], writes=["x_tok"])
                for g in range(4):
                    kb.op("pe", lambda e: e.transpose(pbt[:, g * 128:(g + 1) * 128], B_fm[:, g, csl(c)], id_b),
                          reads=["mx", "cb", "x_tok"], writes=["pbt"], signal=(g == 3))
                kb.op("act", lambda e: e.copy(out=B_toks[pr], in_=pbt[:, 0:512]), reads=["pbt"], writes=[("B_tok", pr)])
                for d in dirs:
                    a_d = adt[:, c, d * 16:(d + 1) * 16]
                    mats = (U_f, SL_f, onesf[:, :]) if d == 0 else (Lo_f, SU_f, onesf[:, :])
                    for q, m in enumerate(mats):
                        kb.op("pe", lambda e: e.matmul(pb[5][:, d * 48 + q * 16:d * 48 + q * 16 + 16], lhsT=m,
                                                       rhs=a_d, start=True, stop=True),
                              reads=["cc", "onesf", "adt"], writes=[("pb", 5)], signal=True)
                ncol = 48 * len(dirs)
                kb.op("act", lambda e: e.activation(out=sstv(pr)[:, 0:ncol], in_=pb[5][:, 0:ncol], func=AF.Exp),
                      reads=[("pb", 5)], writes=[("sst", pr)])

            def bc16(ap16):
                return ap16.unsqueeze(2).to_broadcast([128, 16, 64])

            def v3(ap):
                return ap.rearrange("p (h q) -> p h q", h=16)

            def su_front(c, d, pr, have_xdt=False):
                if not have_xdt:
                    kb.op("dve", lambda e: e.tensor_tensor(out=v3(xdt[d]), in0=v3(x_tok),
                                                           in1=bc16(dtt[:, c, d * 16:(d + 1) * 16]), op=ALU.mult),
                          reads=["x_tok", "dtt"], writes=[("xdt", d)])
                kb.op("dve", lambda e: e.tensor_tensor(out=v3(xdte), in0=v3(xdt[d]),
                                                       in1=bc16(sstv(pr)[:, d * 48 + 16:d * 48 + 32]), op=ALU.mult),
                      reads=[("xdt", d), ("sst", pr)], writes=["xdte"])

            def su_tail(c, d, pr):
                for hf in range(2):
                    for g2 in range(2):
                        g = hf * 2 + g2
                        kb.op("pe", lambda e: e.matmul(pb[3 + hf][:, g2 * 256:(g2 + 1) * 256],
                                                       lhsT=B_toks[pr][:, g * 128:(g + 1) * 128],
                                                       rhs=xdte[:, g * 256:(g + 1) * 256], start=True, stop=True),
                              reads=[("B_tok", pr), "xdte"], writes=[("pb", 3 + hf)], signal=True)
                kb.op("dve", lambda e: e.tensor_tensor(out=v3(Sst[d]), in0=v3(Sst[d]),
                                                       in1=bc16(sstv(pr)[:, d * 48 + 32:d * 48 + 48]), op=ALU.mult),
                      reads=[("S", d), ("sst", pr)], writes=[("S", d)])
                for hf in range(2):
                    kb.op("dve", lambda e: e.tensor_tensor(out=Sst[d][:, fsl(hf)], in0=Sst[d][:, fsl(hf)],
                                                           in1=pb[3 + hf][:, :], op=ALU.add),
                          reads=[("S", d), ("pb", 3 + hf)], writes=[("S", d)])

            def seq_chunks(s_):
                return [s_ * cpl + i for i in range(cpl)]

            def init_state(d, first):
                if first:
                    if ps_id == 0:
                        kb.dma("act", Sst[d], st_in[l, d], writes=[("S", d)], skey=("stin", d))
                    else:
                        kb.op("dve", lambda e: e.memset(Sst[d], 0.0), writes=[("S", d)])
                else:
                    kb.op("dve", lambda e: e.tensor_scalar(out=Sst[d], in0=Sst[d], scalar1=fcol, scalar2=None,
                                                           op0=ALU.mult),
                          reads=[("S", d), "flag"], writes=[("S", d)])

            so = 0 if ps_id == 0 else 4
            order1 = [c for s_ in range(nseq) for c in seq_chunks(s_)]
            order2 = [c for s_ in reversed(range(nseq)) for c in reversed(seq_chunks(s_))]
            pos1 = {c: i for i, c in enumerate(order1)}
            pos2 = {c: i for i, c in enumerate(order2)}
            prep(order1[0], (0,), 0)
            for s in range(nseq):
                chunks = seq_chunks(s)
                init_state(0, s == 0)
                for c in chunks:
                    i1 = pos1[c]
                    pr = i1 % 2
                    bg_step(1)
                    kb.op("act", lambda e: e.copy(out=Sf_all[:, c, :], in_=Sst[0]), reads=[("S", 0)], writes=["sfall"])
                    su_front(c, 0, pr)
                    if i1 + 1 < len(order1):
                        prep(order1[i1 + 1], (0,), 1 - pr)
                    su_tail(c, 0, pr)
                kb.dma("sp", st_out[so + s, l, 0], Sst[0], reads=[("S", 0)], skey=("sto", 0))
            prep(order2[0], (0, 1), 0)
            for s in reversed(range(nseq)):
                chunks = seq_chunks(s)
                init_state(1, s == nseq - 1)
                for c in reversed(chunks):
                    i2 = pos2[c]
                    pr = i2 % 2
                    bg_step(1)
                    kb.op("act", lambda e: e.copy(out=Sbb, in_=Sst[1]), reads=[("S", 1)], writes=["Sbb"])
                    def front():
                        for d in range(2):
                            kb.op("dve", lambda e: e.tensor_tensor(out=v3(xdt[d]), in0=v3(x_tok),
                                                                   in1=bc16(dtt[:, c, d * 16:(d + 1) * 16]), op=ALU.mult),
                                  reads=["x_tok", "dtt"], writes=[("xdt", d)])
                        for g in range(4):
                            kb.op("pe", lambda e: e.matmul(pb[4][:, g * 128:(g + 1) * 128], lhsT=B_fm[:, g, csl(c)],
                                                           rhs=C_fm[:, g, csl(c)], start=True, stop=True),
                                  reads=["mx"], writes=[("pb", 4)], signal=True)
                        g3 = pb[4][:, :].rearrange("p (g l) -> p g l", g=4)
                        for d, mk in ((0, U_f), (1, Lo_f)):
                            kb.op("dve", lambda e: e.tensor_tensor(out=GMv[d], in0=g3,
                                                                   in1=mk.unsqueeze(1).to_broadcast([128, 4, 128]),
                                                                   op=ALU.mult),
                                  reads=[("pb", 4), "cc"], writes=[("GM", d)])
                        kb.op("dve", lambda e: e.tensor_tensor(out=v3(xdte), in0=v3(x_tok), in1=bc16(Dbc), op=ALU.mult),
                              reads=["x_tok", "pp"], writes=["xdte"])
                        for hf in range(2):
                            kb.op("pe", lambda e: e.matmul(pb[hf][:, :], lhsT=id_b, rhs=xdte[:, fsl(hf)], start=True,
                                                           stop=False),
                                  reads=["xdte", "cb"], writes=[("pb", hf)], signal=True)


                    def stA1(u, cc=None):
                        cc = c if cc is None else cc
                        g, d = divmod(u, 2)
                        mk, lm = (U_f, SL_f) if d == 0 else (Lo_f, SU_f)
                        hh = d * 16 + g * 4
                        sbk = 2 if d == 0 else 5
                        kb.op("dve", lambda e: e.tensor_tensor(
                            out=rsegs[d][:, :, :], in0=mk.unsqueeze(1).to_broadcast([128, 4, 128]),
                            in1=adt[:, cc, hh:hh + 4].unsqueeze(2).to_broadcast([128, 4, 128]), op=ALU.mult),
                            reads=["cc", "adt"], writes=[("rseg", d)])
                        kb.op("pe", lambda e: e.matmul(pb[sbk][:, :], lhsT=lm,
                                                       rhs=rsegs[d][:, :, :].rearrange("p r l -> p (r l)"),
                                                       start=True, stop=True),
                              reads=[("rseg", d), "cc"], writes=[("pb", sbk)], signal=True)

                    def stA2(u):
                        g, d = divmod(u, 2)
                        sbk = 2 if d == 0 else 5
                        kb.op("act", lambda e: e.activation(out=decb[d][:, :, :].rearrange("p r l -> p (r l)"),
                                                            in_=pb[sbk][:, :], func=AF.Exp),
                              reads=[("pb", sbk)], writes=[("dec", d)])

                    def stB(u):
                        g, d = divmod(u, 2)
                        kb.op("dve", lambda e: e.tensor_tensor(out=mtb[d][:, :, :], in0=decb[d][:, :, :],
                                                               in1=GMv[d][:, g:g + 1, :].to_broadcast([128, 4, 128]),
                                                               op=ALU.mult),
                              reads=[("dec", d), ("GM", d)], writes=[("mt", d)])

                    def stC(g):
                        for r in range(4):
                            h = g * 4 + r
                            yb = pb[h // 8]
                            col = (h % 8) * 64
                            for d in range(2):
                                kb.op("pe", lambda e: e.matmul(yb[:, col:col + 64], lhsT=mtb[d][:, r, :],
                                                               rhs=xdt[d][:, h * 64:(h + 1) * 64], start=False,
                                                               stop=(d == 1 and h % 8 == 7)),
                                      reads=[("mt", d), ("xdt", d)], writes=[("pb", h // 8)], signal=True)

                    if i2 == 0:
                        stA1(0); stA1(1)
                    front()
                    stA2(0); stA2(1)
                    for g in range(4):
                        if g < 3:
                            stA1(2 * g + 2); stA1(2 * g + 3)
                        stB(2 * g); stB(2 * g + 1)
                        if g < 3:
                            stA2(2 * g + 2); stA2(2 * g + 3)
                        stC(g)
                    bg_step(2 if i2 < 3 else 1)
                    for d in range(2):
                        sb_src = Sf_all[:, c, :] if d == 0 else Sbb
                        for hf in range(2):
                            for g2 in range(2):
                                g = hf * 2 + g2
                                kb.op("pe", lambda e: e.matmul(pb[3 + hf][:, g2 * 256:(g2 + 1) * 256],
                                                               lhsT=C_fm[:, g, csl(c)],
                                                               rhs=sb_src[:, g * 256:(g + 1) * 256],
                                                               start=True, stop=True),
                                      reads=["mx", "sfall", "Sbb"], writes=[("pb", 3 + hf)], signal=True)
                        for hf in range(2):
                            ev = sstv(pr)[:, d * 48 + hf * 8:d * 48 + hf * 8 + 8].unsqueeze(2).to_broadcast([128, 8, 64])
                            src = pb[3 + hf][:, :].rearrange("p (h q) -> p h q", h=8)
                            dst = ysb[:, fsl(hf)].rearrange("p (h q) -> p h q", h=8)
                            if d == 0:
                                kb.op("dve", lambda e: e.tensor_tensor(out=dst, in0=src, in1=ev, op=ALU.mult),
                                      reads=[("pb", 3 + hf), ("sst", pr)], writes=["ysb"])
                            else:
                                i = ntmp()
                                tv = tmpf[i][:, :].rearrange("p (h q) -> p h q", h=8)
                                kb.op("dve", lambda e: e.tensor_tensor(out=tv, in0=src, in1=ev, op=ALU.mult),
                                      reads=[("pb", 3 + hf), ("sst", pr)], writes=[("tmpf", i)])
                                kb.op("dve", lambda e: e.tensor_tensor(out=ysb[:, fsl(hf)], in0=ysb[:, fsl(hf)],
                                                                       in1=tmpf[i][:, :], op=ALU.add),
                                      reads=[("tmpf", i), "ysb"], writes=["ysb"])
                    if i2 + 1 < len(order2):
                        prep(order2[i2 + 1], (0, 1), 1 - pr)
                        stA1(0, order2[i2 + 1]); stA1(1, order2[i2 + 1])
                    for hf in range(2):
                        kb.op("dve", lambda e: e.tensor_tensor(out=ysb[:, fsl(hf)], in0=ysb[:, fsl(hf)],
                                                               in1=pb[hf][:, :], op=ALU.add),
                              reads=["ysb", ("pb", hf)], writes=["ysb"])
                    if c == 0:
                        dump("ysb0", ysb, ["ysb"])
                    kb.op("dve", lambda e: e.tensor_tensor(out=ttb, in0=ysb, in1=zs[:, c, :], op=ALU.mult),
                          reads=["ysb", ("zs", c)], writes=["ttb"])
                    su_front(c, 1, pr, have_xdt=True)
                    su_tail(c, 1, pr)
                    for k in range(8):
                        kb.op("pe", lambda e: e.transpose(pbt[:, k * 128:(k + 1) * 128], ttb[:, k * 128:(k + 1) * 128],
                                                          id_b),
                              reads=["ttb", "cb"], writes=["pbt"], signal=(k == 7))
                    kb.op("act", lambda e: e.copy(out=xs_fm[:, :, csl(c)],
                                                  in_=pbt[:, :].rearrange("p (k t) -> p k t", k=8)),
                          reads=["pbt"], writes=["mx"])
                kb.dma("sp", st_out[so + s, l, 1], Sst[1], reads=[("S", 1)], skey=("sto", 1))
            bg_drain()
            kb.barrier()
            dump("t_fm", xs_fm[:, :, :], ["mx"])
            make_h(3, use_stash=rstash)

            def merge_branch(bi, wbr, KCb, feat_fn, feat_keys, use_rstd, scale_cols):
                for half in range(2):
                    if KCb == 8:
                        wv, wk = wblock(wbr, 0, 8, half * 512, 512)
                    else:
                        wv, wk = wblock(wbr, 0, 4, half * 512, 512)
                    gv, gk = wblock(wi, 0, 8, C_GATE + bi * 1024 + half * 512, 512)
                    for tt in range(P["NT"]):
                        for mc in range(4):
                            mo = half * 4 + mc
                            bb, bg = nbank(), nbank()
                            for kc in range(KCb):
                                kb.op("pe", lambda e: e.matmul(pb[bb][:, 0:P["TW"]], lhsT=wv[:, kc, csl(mc)], rhs=feat_fn(kc, tt),
                                                               start=(kc == 0), stop=(kc == KCb - 1)),
                                      reads=[wk] + feat_keys, writes=[("pb", bb)], signal=(kc == KCb - 1))
                            for kc in range(8):
                                kb.op("pe", lambda e: e.matmul(pb[bg][:, 0:P["TW"]], lhsT=gv[:, kc, csl(mc)],
                                                               rhs=hbuf[:, kc, tsl(tt)], start=(kc == 0), stop=(kc == 7)),
                                      reads=[gk, ("h", tt)], writes=[("pb", bg)], signal=(kc == 7))
                            i = ntmp()
                            kb.op("act", lambda e: e.activation(out=tmpf[i][:, 0:P["TW"]], in_=pb[bg][:, 0:P["TW"]], func=AF.Sigmoid),
                                  reads=[("pb", bg)], writes=[("tmpf", i)])
                            kb.op("dve", lambda e: e.tensor_tensor(out=tmpf[i][:, 0:P["TW"]], in0=tmpf[i][:, 0:P["TW"]], in1=pb[bb][:, 0:P["TW"]],
                                                                   op=ALU.mult),
                                  reads=[("tmpf", i), ("pb", bb)], writes=[("tmpf", i)])
                            if bi == 0:
                                kb.op("dve", lambda e: e.tensor_tensor(out=merged[:, mo, tsl(tt)], in0=tmpf[i][:, 0:P["TW"]],
                                                                       in1=rstd[tt][:, 0:P["TW"]], op=ALU.mult),
                                      reads=[("tmpf", i), ("rstd", tt)], writes=[("mg", tt)])
                            elif bi == 1:
                                kb.op("dve", lambda e: e.tensor_tensor(out=merged[:, mo, tsl(tt)],
                                                                       in0=merged[:, mo, tsl(tt)], in1=tmpf[i][:, 0:P["TW"]],
                                                                       op=ALU.add),
                                      reads=[("tmpf", i), ("mg", tt)], writes=[("mg", tt)])
                            else:
                                kb.op("dve", lambda e: e.tensor_tensor(out=mgb16[:, mo, tsl(tt)],
                                                                       in0=merged[:, mo, tsl(tt)], in1=tmpf[i][:, 0:P["TW"]],
                                                                       op=ALU.add),
                                      reads=[("tmpf", i), ("mg", tt)], writes=[("mgb", tt)])

            for tt in range(P["NT"]):
                rms_stats(lambda kc, t_: xs_fm[:, kc, tsl(t_)], lambda kc, t_: ["mx"], 8, 1024, tt)
            for kc in range(8):
                kb.op("act", lambda e: e.activation(out=xs_fm[:, kc, 0:P["T"]], in_=xs_fm[:, kc, 0:P["T"]], func=AF.Identity,
                                                    scale=ppc(l, PP_SNG + kc, 1)),
                      reads=["mx", "pp"], writes=["mx"])
            merge_branch(0, w_br_ssd[l], 8, lambda kc, tt: xs_fm[:, kc, tsl(tt)], ["mx"], True, None)
            dump("merged0", merged[:, :, :], [("mg", 0), ("mg", 1)])
            PWc = L + 30
            vpad = R_MX[:, 0:2 * nseq * PWc].bitcast(BF16).rearrange("p (k s t) -> p k s t", k=4, s=nseq)
            cacc = R_MX[:, 4600:4600 + 4096].rearrange("p (k t) -> p k t", k=4)
            cv = R_MX[:, 8700:8700 + 2048].bitcast(BF16).rearrange("p (k t) -> p k t", k=4)
            kb.op("dve", lambda e: e.memset(R_MX[:, 0:2 * nseq * PWc].bitcast(BF16), 0.0), writes=["vpad"])
            for half in range(2):
                wa, ka = wblock(wi, 0, 8, C_CONF + half * 256, 256)
                wg_, kg = wblock(wi, 0, 8, C_CONF + 512 + half * 256, 256)
                for tt in range(P["NT"]):
                    for mc in range(2):
                        ch = half * 2 + mc
                        ba, bg = nbank(), nbank()
                        for kc in range(8):
                            kb.op("pe", lambda e: e.matmul(pb[ba][:, 0:P["TW"]], lhsT=wa[:, kc, csl(mc)], rhs=hbuf[:, kc, tsl(tt)],
                                                           start=(kc == 0), stop=(kc == 7)),
                                  reads=[ka, ("h", tt)], writes=[("pb", ba)], signal=(kc == 7))
                        for kc in range(8):
                            kb.op("pe", lambda e: e.matmul(pb[bg][:, 0:P["TW"]], lhsT=wg_[:, kc, csl(mc)], rhs=hbuf[:, kc, tsl(tt)],
                                                           start=(kc == 0), stop=(kc == 7)),
                                  reads=[kg, ("h", tt)], writes=[("pb", bg)], signal=(kc == 7))
                        i = ntmp()
                        kb.op("act", lambda e: e.activation(out=tmpf[i][:, 0:P["TW"]], in_=pb[bg][:, 0:P["TW"]], func=AF.Sigmoid),
                              reads=[("pb", bg)], writes=[("tmpf", i)])
                        if L <= P["TW"]:
                            spt = P["TW"] // L
                            dst = vpad[:, ch, tt * spt:(tt + 1) * spt, 15:15 + L]
                            s0 = tmpf[i][:, 0:P["TW"]].rearrange("p (s t) -> p s t", s=spt)
                            s1 = pb[ba][:, 0:P["TW"]].rearrange("p (s t) -> p s t", s=spt)
                        else:
                            dst = vpad[:, ch, 0, 15 + tt * 512:15 + (tt + 1) * 512]
                            s0 = tmpf[i][:, 0:P["TW"]]
                            s1 = pb[ba][:, 0:P["TW"]]
                        kb.op("dve", lambda e: e.tensor_tensor(out=dst, in0=s0, in1=s1, op=ALU.mult),
                              reads=[("tmpf", i), ("pb", ba)], writes=["vpad"])
            if nseq > 1:
                for ch in range(4):
                    kb.op("dve", lambda e: e.tensor_scalar(out=vpad[:, ch, 1:nseq, 0:15], in0=vpad[:, ch, 0:nseq - 1, L:L + 15],
                                                           scalar1=fcol, scalar2=None, op0=ALU.mult),
                          reads=["vpad", "flag"], writes=["vpad"])
                    kb.op("dve", lambda e: e.tensor_scalar(out=vpad[:, ch, 0:nseq - 1, L + 15:L + 30],
                                                           in0=vpad[:, ch, 1:nseq, 15:30], scalar1=fcol, scalar2=None,
                                                           op0=ALU.mult),
                          reads=["vpad", "flag"], writes=["vpad"])
            dq = 0
            for ch in range(4):
                banks = [nbank(), nbank()]
                for k in range(31):
                    q = dq % 4
                    dq += 1
                    kb.op("dve", lambda e: e.tensor_scalar(out=dgb[q][:], in0=id_b, scalar1=ppc(l, PP_CCW + ch * 31 + k, 1),
                                                           scalar2=None, op0=ALU.mult),
                          reads=["cb", "pp"], writes=[("dg", q)])
                    for tt in range(P["NT"]):
                        if L <= P["TW"]:
                            spt = P["TW"] // L
                            rhs = vpad[:, ch, tt * spt:(tt + 1) * spt, k:k + L]
                        else:
                            rhs = vpad[:, ch, 0, k + tt * 512:k + (tt + 1) * 512]
                        kb.op("pe", lambda e: e.matmul(pb[banks[tt]][:, 0:P["TW"]], lhsT=dgb[q][:], rhs=rhs, start=(k == 0),
                                                       stop=(k == 30)),
                              reads=[("dg", q), "vpad"], writes=[("pb", banks[tt])], signal=True)
                for tt in range(P["NT"]):
                    kb.op("act", lambda e: e.activation(out=cacc[:, ch, tsl(tt)], in_=pb[banks[tt]][:, 0:P["TW"]], func=AF.Identity,
                                                        bias=ppc(l, PP_CCB + ch, 1), scale=1.0),
                          reads=[("pb", banks[tt]), "pp"], writes=[("cacc", ch)])
            dump("cconv", cacc[:, :, :], [("cacc", ch) for ch in range(4)])
            for tt in range(P["NT"]):
                for ch in range(4):
                    kb.op("pe", lambda e: e.matmul(pb[6][:, 0:P["TW"]], lhsT=onesf[:, :], rhs=cacc[:, ch, tsl(tt)], start=(ch == 0),
                                                   stop=(ch == 3)),
                          reads=[("cacc", ch), "onesf"], writes=[("pb", 6)], signal=True)
                i1, i2 = ntmp(), ntmp()
                for ch in range(4):
                    kb.op("act", lambda e: e.activation(out=tmpf[i1][:, 0:P["TW"]], in_=cacc[:, ch, tsl(tt)], func=AF.Square),
                          reads=[("cacc", ch)], writes=[("tmpf", i1)])
                    kb.op("pe", lambda e: e.matmul(pb[5][:, 0:P["TW"]], lhsT=onesf[:, :], rhs=tmpf[i1][:, 0:P["TW"]], start=(ch == 0),
                                                   stop=(ch == 3)),
                          reads=[("tmpf", i1), "onesf"], writes=[("pb", 5)], signal=True)
                kb.op("dve", lambda e: e.tensor_scalar(out=tmpf[i2][:, 0:P["TW"]], in0=pb[6][:, 0:P["TW"]], scalar1=1.0 / 512, scalar2=None,
                                                       op0=ALU.mult), reads=[("pb", 6)], writes=[("tmpf", i2)])
                kb.op("dve", lambda e: e.tensor_tensor(out=tmpf[i1][:, 0:P["TW"]], in0=tmpf[i2][:, 0:P["TW"]], in1=tmpf[i2][:, 0:P["TW"]], op=ALU.mult),
                      reads=[("tmpf", i2)], writes=[("tmpf", i1)])
                kb.op("dve", lambda e: e.scalar_tensor_tensor(out=rstd[tt][:, 0:P["TW"]], in0=pb[5][:, 0:P["TW"]], scalar=1.0 / 512,
                                                              in1=tmpf[i1][:, 0:P["TW"]], op0=ALU.mult, op1=ALU.subtract),
                      reads=[("pb", 5), ("tmpf", i1)], writes=[("rstd", tt)])
                kb.op("act", lambda e: e.activation(out=rstd[tt][:, 0:P["TW"]], in_=rstd[tt][:, 0:P["TW"]], func=AF.Ln, bias=EPS, scale=1.0),
                      reads=[("rstd", tt)], writes=[("rstd", tt)])
                kb.op("act", lambda e: e.activation(out=rstd[tt][:, 0:P["TW"]], in_=rstd[tt][:, 0:P["TW"]], func=AF.Exp, scale=-0.5),
                      reads=[("rstd", tt)], writes=[("rstd", tt)])
                for ch in range(4):
                    kb.op("dve", lambda e: e.tensor_tensor(out=cacc[:, ch, tsl(tt)], in0=cacc[:, ch, tsl(tt)],
                                                           in1=tmpf[i2][:, 0:P["TW"]], op=ALU.subtract),
                          reads=[("cacc", ch), ("tmpf", i2)], writes=[("cacc", ch)])
                    kb.op("dve", lambda e: e.tensor_tensor(out=cacc[:, ch, tsl(tt)], in0=cacc[:, ch, tsl(tt)],
                                                           in1=rstd[tt][:, 0:P["TW"]], op=ALU.mult),
                          reads=[("cacc", ch), ("rstd", tt)], writes=[("cacc", ch)])
                    kb.op("act", lambda e: e.activation(out=cv[:, ch, tsl(tt)], in_=cacc[:, ch, tsl(tt)], func=AF.Silu,
                                                        bias=ppc(l, PP_CLB + ch, 1), scale=ppc(l, PP_CLG + ch, 1)),
                          reads=[("cacc", ch), "pp"], writes=["cv"])
            merge_branch(1, w_br_conf[l], 4, lambda kc, tt: cv[:, kc, tsl(tt)], ["cv"], False, None)
            PWp = L + 16
            NP_ = nseq * PWp
            upad = R_MX[:, 0:NP_].rearrange("p (s t) -> p s t", s=nseq)
            sA = R_MX[:, 1200:1200 + NP_].rearrange("p (s t) -> p s t", s=nseq)
            sB = R_MX[:, 2400:2400 + NP_].rearrange("p (s t) -> p s t", s=nseq)
            opad = R_MX[:, 3600:3600 + NP_].rearrange("p (s t) -> p s t", s=nseq)
            cA = R_MX[:, 4800:4800 + NP_].rearrange("p (s t) -> p s t", s=nseq)
            cB = R_MX[:, 6000:6000 + NP_].rearrange("p (s t) -> p s t", s=nseq)
            pooled = R_MX[:, 7200:7200 + 2048].bitcast(BF16).rearrange("p (k t) -> p k t", k=4)
            mixed = R_MX[:, 9300:9300 + 2048].bitcast(BF16).rearrange("p (k t) -> p k t", k=4)
            kb.op("dve", lambda e: e.memset(R_MX[:, 0:NP_], 0.0), writes=["upad"])
            kb.op("dve", lambda e: e.memset(R_MX[:, 3600:3600 + NP_], 0.0), writes=["opad"])
            kb.op("dve", lambda e: e.memset(opad[:, :, 8:8 + L], 1.0), writes=["opad"])
            if nseq > 1:
                kb.op("dve", lambda e: e.tensor_scalar(out=opad[:, 1:nseq, 0:8], in0=opad[:, 0:nseq - 1, 8:16], scalar1=fcol,
                                                       scalar2=None, op0=ALU.mult), reads=["opad", "flag"], writes=["opad"])
                kb.op("dve", lambda e: e.tensor_scalar(out=opad[:, 0:nseq - 1, L + 8:L + 16], in0=opad[:, 1:nseq, 8:16],
                                                       scalar1=fcol, scalar2=None, op0=ALU.mult),
                      reads=["opad", "flag"], writes=["opad"])
            wv, wk = wblock(wi, 0, 8, C_POOL, 512)
            pwv, pwk = wload([(lambda s: s[:, 0:512].rearrange("p (g d) -> p g d", g=4),
                               pool_w[l].rearrange("g c d -> c g d"))])
            pwv = pwv[:, 0:512].rearrange("p (g d) -> p g d", g=4)
            for gi in range(4):
                w_ = 2 << gi
                for tt in range(P["NT"]):
                    b = nbank()
                    for kc in range(8):
                        kb.op("pe", lambda e: e.matmul(pb[b][:, 0:P["TW"]], lhsT=wv[:, kc, csl(gi)], rhs=hbuf[:, kc, tsl(tt)],
                                                       start=(kc == 0), stop=(kc == 7)),
                              reads=[wk, ("h", tt)], writes=[("pb", b)], signal=(kc == 7))
                    if L <= P["TW"]:
                        spt = P["TW"] // L
                        dst = upad[:, tt * spt:(tt + 1) * spt, 8:8 + L]
                        src = pb[b][:, 0:P["TW"]].rearrange("p (s t) -> p s t", s=spt)
                    else:
                        dst = upad[:, 0, 8 + tt * 512:8 + (tt + 1) * 512]
                        src = pb[b][:, 0:P["TW"]]
                    kb.op("act", lambda e: e.copy(out=dst, in_=src), reads=[("pb", b)], writes=["upad"])
                if nseq > 1:
                    kb.op("dve", lambda e: e.tensor_scalar(out=upad[:, 1:nseq, 0:8], in0=upad[:, 0:nseq - 1, L:L + 8],
                                                           scalar1=fcol, scalar2=None, op0=ALU.mult),
                          reads=["upad", "flag"], writes=["upad"])
                    kb.op("dve", lambda e: e.tensor_scalar(out=upad[:, 0:nseq - 1, L + 8:L + 16], in0=upad[:, 1:nseq, 8:16],
                                                           scalar1=fcol, scalar2=None, op0=ALU.mult),
                          reads=["upad", "flag"], writes=["upad"])
                for (src0, bA, bB, kn) in ((upad, sA, sB, "ps"), (opad, cA, cB, "pc")):
                    cur, n, step = src0, PWp, 1
                    bufs2 = [bA, bB]
                    bi2 = 0
                    while step < w_:
                        n2 = n - step
                        dstb = bufs2[bi2]
                        kb.op("dve", lambda e: e.tensor_tensor(out=dstb[:, :, 0:n2], in0=cur[:, :, 0:n2],
                                                               in1=cur[:, :, step:step + n2], op=ALU.add),
                              reads=["upad", "opad", (kn, 0), (kn, 1)], writes=[(kn, bi2)])
                        cur, n, step = dstb, n2, step * 2
                        bi2 ^= 1
                    if kn == "ps":
                        sumv = cur
                    else:
                        cntv = cur
                o0 = 8 - w_ // 2
                i = ntmp()
                i2 = ntmp()
                for tt in range(P["NT"]):
                    if L <= P["TW"]:
                        spt = P["TW"] // L
                        def win(v):
                            return v[:, tt * spt:(tt + 1) * spt, o0:o0 + L]
                        uview = upad[:, tt * spt:(tt + 1) * spt, 8:8 + L]
                        t3 = lambda a: a.rearrange("p (s t) -> p s t", s=spt)
                    else:
                        def win(v):
                            return v[:, 0, o0 + tt * 512:o0 + (tt + 1) * 512]
                        uview = upad[:, 0, 8 + tt * 512:8 + (tt + 1) * 512]
                        t3 = lambda a: a
                    kb.op("act", lambda e: e.activation(out=t3(tmpf[i][:, 0:P["TW"]]), in_=win(cntv), func=AF.Ln),
                          reads=[("pc", 0), ("pc", 1)], writes=[("tmpf", i)])
                    kb.op("act", lambda e: e.activation(out=tmpf[i][:, 0:P["TW"]], in_=tmpf[i][:, 0:P["TW"]], func=AF.Exp,
                                                        scale=-1.0),
                          reads=[("tmpf", i)], writes=[("tmpf", i)])
                    kb.op("dve", lambda e: e.tensor_tensor(out=t3(tmpf[i][:, 0:P["TW"]]), in0=t3(tmpf[i][:, 0:P["TW"]]), in1=win(sumv),
                                                           op=ALU.mult),
                          reads=[("tmpf", i), ("ps", 0), ("ps", 1)], writes=[("tmpf", i)])
                    kb.op("dve", lambda e: e.tensor_tensor(out=t3(pooled[:, gi, tsl(tt)]), in0=t3(tmpf[i][:, 0:P["TW"]]),
                                                           in1=uview, op=ALU.subtract),
                          reads=[("tmpf", i), "upad"], writes=["pooled"])
                    b = nbank()
                    kb.op("pe", lambda e: e.matmul(pb[b][:, 0:P["TW"]], lhsT=pwv[:, gi, :], rhs=pooled[:, gi, tsl(tt)],
                                                   start=True, stop=True),
                          reads=[pwk, "pooled"], writes=[("pb", b)], signal=True)
                    kb.op("act", lambda e: e.activation(out=mixed[:, gi, tsl(tt)], in_=pb[b][:, 0:P["TW"]], func=AF.Identity,
                                                        scale=ppc(l, PP_PSC + gi, 1)),
                          reads=[("pb", b), "pp"], writes=["mixed"])
            merge_branch(2, w_br_pool[l], 4, lambda kc, tt: mixed[:, kc, tsl(tt)], ["mixed"], False, None)
            mgb = mgb16
            fs = FusedStats(8)
            for half in range(2):
                wv, wk = wblock(w_out[l], 0, 8, half * 512, 512)
                for tt in range(P["NT"]):
                    for mc in range(4):
                        mo = half * 4 + mc
                        b = nbank()
                        for kc in range(8):
                            kb.op("pe", lambda e: e.matmul(pb[b][:, 0:P["TW"]], lhsT=wv[:, kc, csl(mc)], rhs=mgb[:, kc, tsl(tt)],
                                                           start=(kc == 0), stop=(kc == 7)),
                                  reads=[wk, ("mgb", tt)], writes=[("pb", b)], signal=(kc == 7))
                        kb.op("act", lambda e: e.copy(out=merged[:, mo, tsl(tt)], in_=pb[b][:, 0:P["TW"]]),
                              reads=[("pb", b)], writes=[("mg", tt)])
                        fs.add(b, tt)
            fs.finish(D)
            resid_update(5, lambda mo, t_: merged[:, mo, tsl(t_)], lambda mo, t_: [("mg", t_)])
            kb.barrier()
            dump("x_mix", xres[:, :, :], ALLX)

        ppl_loaded = {"l": None}
        for ps_id in passes:
            kb.barrier()
            xstat["ok"] = False
            if len(passes) == 2:
                wcache["mode"] = "fill" if ps_id == passes[0] else "use"
                wcache["count"][0 if ps_id == passes[0] else 1] = 0
                wcache["n"] = 0
            else:
                wcache["mode"] = "off"
            if ps_id == 0:
                P.update(T=1024, TW=512, NT=2, NCH=8)
            else:
                P.update(T=256, TW=256, NT=1, NCH=2)
                kb.barrier(engs=("pe", "act", "dve", "pool", "sp"))
                xres = R_X[:, 0:2048].rearrange("p (k t) -> p k t", k=8)
                for j in range(3):
                    slots.append(R_X[:, 2048 * (j + 1):2048 * (j + 2)].bitcast(BF16))
                NSLOT = len(slots)
            kb.dma("sp", xres[:, :, 0:P["T"]], xT_in[ps_id].rearrange("(kc p) t -> p kc t", p=128), writes=ALLX,
                   skey="xin")
            if ps_id == passes[0]:
                kb.op("act", lambda e: e.activation(out=scond[:, :, :],
                                                    in_=condsb[:, :].rearrange("p (k c) -> p k c", c=2), func=AF.Silu),
                      reads=["cond"], writes=["scond"])
            if ps_id == 0:
                post = R_A[:, 0:2050]
                kb.dma("act", post, pos_d, writes=["post"], skey="pos")
                rpos = R_A[:, 0:1024]
                cpos = R_A[:, 1024:2048]
                ang = R_A[:, 3300:3300 + 1024]
                kk = R_A[:, 4500:4500 + 1024]
                om = R_A[:, 5600:5602]
                kb.op("act", lambda e: e.activation(out=om, in_=R_A[:, 2048:2050], func=AF.Exp,
                                                    scale=-float(np.log(10000.0)) / 256.0),
                      reads=["post"], writes=["om"])
                PI = float(np.pi)
                MAGIC = 12582912.0
                for kc in range(8):
                    posv = rpos if kc < 4 else cpos
                    shift = 0.0 if (kc % 4) < 2 else PI / 2
                    kb.op("dve", lambda e: e.tensor_scalar(out=ang, in0=posv, scalar1=om[:, kc % 2:kc % 2 + 1],
                                                           scalar2=shift, op0=ALU.mult, op1=ALU.add),
                          reads=["post", "om"], writes=["ang"])
                    kb.op("dve", lambda e: e.tensor_scalar(out=kk, in0=ang, scalar1=1.0 / (2 * PI), scalar2=MAGIC,
                                                           op0=ALU.mult, op1=ALU.add),
                          reads=["ang"], writes=["kk"])
                    kb.op("dve", lambda e: e.tensor_scalar(out=kk, in0=kk, scalar1=-MAGIC, scalar2=None, op0=ALU.add),
                          reads=["kk"], writes=["kk"])
                    kb.op("dve", lambda e: e.scalar_tensor_tensor(out=ang, in0=kk, scalar=-2 * PI, in1=ang,
                                                                  op0=ALU.mult, op1=ALU.add),
                          reads=["kk", "ang"], writes=["ang"])
                    kb.op("act", lambda e: e.activation(out=ang, in_=ang, func=AF.Sin), reads=["ang"], writes=["ang"])
                    kb.op("dve", lambda e: e.scalar_tensor_tensor(out=xres[:, kc, :], in0=ang, scalar=fcol,
                                                                  in1=xres[:, kc, :], op0=ALU.mult, op1=ALU.add),
                          reads=["ang", "flag"] + ALLX, writes=ALLX)
                kb.barrier()
            dump(f"x0_{ps_id}", xres[:, :, :], ALLX)
            for l in range(n_layers):
                if ppl_loaded["l"] != (ps_id, l):
                    kb.dma("sp", ppt[l % 2][:], pp_d[:, l * PPL:(l + 1) * PPL], writes=["pp"], skey=("pp", l % 2))
                    ppl_loaded["l"] = (ps_id, l)
                if ps_id == passes[0]:
                    if l == 0:
                        bg["gen"] = mod_steps(0)
                        bg_drain()
                    if l + 1 < n_layers:
                        kb.dma("sp", ppt[(l + 1) % 2][:], pp_d[:, (l + 1) * PPL:(l + 2) * PPL], writes=["pp"],
                               skey=("pp", (l + 1) % 2))
                        ppl_loaded["l"] = (ps_id, l + 1)
                        bg["gen"] = mod_steps(l + 1)
                cur["mc"] = mcall[:, (ps_id * DEPTH + l) * 72:(ps_id * DEPTH + l + 1) * 72]
                wcache["l"] = l
                wcache["n"] = 0
                ffn(l, 0)
                mixer(l, ps_id)
                ffn(l, 1)
            kb.dma("sp", yT_out[ps_id].rearrange("(kc p) t -> p kc t", p=128), xres[:, :, 0:P["T"]], reads=ALLX,
                   skey="yout")
        kb.finish("sp")
    return nc


def _pack_inputs(inp):
    f = np.float32
    ar = lambda k: np.asarray(inp[k], dtype=f)
    pp = np.zeros((128, DEPTH, PPL), f)
    fm = lambda v: v.reshape(-1, 128).T
    for l in range(DEPTH):
        for i in range(6):
            pp[:, l, PP_NG + i * 8:PP_NG + i * 8 + 8] = fm(ar("norm_g")[l, i])
        pp[:, l, PP_BMOD:PP_BMOD + 72] = fm(ar("b_mod")[l])
        scw = ar("ssd_conv_w")[l]
        pp[:, l, PP_SCW:PP_SCW + 80] = scw.reshape(5, 16, 128).transpose(2, 1, 0).reshape(128, 80)
        pp[:, l, PP_SCB:PP_SCB + 16] = fm(ar("ssd_conv_b")[l])
        pp[:, l, PP_SNG:PP_SNG + 8] = fm(ar("ssd_norm_g")[l])
        ccw = ar("conf_conv_w")[l]
        pp[:, l, PP_CCW:PP_CCW + 124] = ccw.reshape(31, 4, 128).transpose(2, 1, 0).reshape(128, 124)
        pp[:, l, PP_CCB:PP_CCB + 4] = fm(ar("conf_conv_b")[l])
        pp[:, l, PP_CLG:PP_CLG + 4] = fm(ar("conf_ln_g")[l])
        pp[:, l, PP_CLB:PP_CLB + 4] = fm(ar("conf_ln_b")[l])
        pp[:, l, PP_PSC:PP_PSC + 4] = fm(ar("pool_scale")[l])
        pp[:, l, PP_ALOG:PP_ALOG + 32] = ar("ssd_a_log")[l].reshape(1, 32)
        pp[:, l, PP_DTB:PP_DTB + 32] = ar("ssd_dt_bias")[l].reshape(1, 32)
        pp[:, l, PP_DD:PP_DD + 16] = ar("ssd_d")[l].reshape(1, 16)
    pp = np.ascontiguousarray(pp.reshape(128, DEPTH * PPL))
    k = np.arange(128)
    cc = np.zeros((128, CC_N), f)
    cc[:, CC_U:CC_U + 128] = (k[:, None] <= k[None, :])
    cc[:, CC_LO:CC_LO + 128] = (k[:, None] >= k[None, :])
    cc[:, CC_SU:CC_SU + 128] = (k[:, None] < k[None, :])
    cc[:, CC_SL:CC_SL + 128] = (k[:, None] > k[None, :])
    cc[:, CC_ID:CC_ID + 128] = np.eye(128)
    pos = np.zeros((128, 2050), f)
    tt = np.arange(1024)
    pos[:, 0:1024] = (tt // 64)[None, :]
    pos[:, 1024:2048] = (tt % 64)[None, :]
    pos[:, 2048] = k
    pos[:, 2049] = k + 128
    return pp, cc, pos


def _core_plan(i):
    if i < 2:
        return i, None, 30 + i
    base = 5 * (i - 2)
    return None, list(range(base, base + 4)), base + 4


def _core_inputs(inp, i, shared):
    f = np.float32
    xp, xs, st = inp["x_prompt"], inp["x_sample"], inp["state_ssd"]
    c, c_ctx = inp["c"], inp["c_ctx"]
    b, pa, pb_ = _core_plan(i)
    m = dict(shared)
    if b is not None:
        m["xT_a"] = np.ascontiguousarray(xs[b].T, dtype=f)
        m["st_in"] = np.ascontiguousarray(st[b].reshape(DEPTH, 2, 1024, 128).transpose(0, 1, 3, 2), dtype=f)
        conda = c[b]
        flag = 1.0
    else:
        m["xT_a"] = np.ascontiguousarray(xp[pa[0]:pa[0] + 4].reshape(T, D).T, dtype=f)
        m["st_in"] = np.zeros((DEPTH, 2, 128, 1024), f)
        conda = c_ctx
        flag = 0.0
    m["xT_b"] = np.ascontiguousarray(xp[pb_].T, dtype=f)
    cond = np.stack([conda, c_ctx], axis=-1)
    m["condT"] = np.ascontiguousarray(cond.reshape(8, 128, 2).transpose(1, 0, 2).reshape(128, 16), dtype=f)
    m["flag"] = np.full((128, 2), flag, f)
    return m


def kernel(**inp):
    f = np.float32
    inp = {k: np.asarray(v, dtype=f) for k, v in inp.items()}
    pp, cc, pos = _pack_inputs(inp)
    shared = {k: np.ascontiguousarray(inp[k]) for k in ("w_mod", "w_ffn_in", "w_ffn_out", "w_in", "w_br_ssd", "w_br_conf",
                                                        "w_br_pool", "pool_w", "w_out")}
    shared.update(pp=pp, cc=cc, pos=pos)
    in_maps = [_core_inputs(inp, i, shared) for i in range(NCORES)]
    nc = build_program()
    res = run_bass_kernel_spmd(nc, in_maps, core_ids=list(range(NCORES)))
    R = res.results
    y_p = np.zeros((32, 256, D), f)
    y_s = np.zeros((2, 1024, D), f)
    ns = np.zeros((32, DEPTH, 2, 16, 64, 128), f)
    for i in range(NCORES):
        b, pa, pb_ = _core_plan(i)
        so = R[i]["st_out"].transpose(0, 1, 2, 4, 3).reshape(5, DEPTH, 2, 16, 64, 128)
        if b is not None:
            y_s[b] = R[i]["yT_a"].T
        else:
            y_p[pa[0]:pa[0] + 4] = R[i]["yT_a"].T.reshape(4, 256, D)
            ns[pa[0]:pa[0] + 4] = so[0:4]
        y_p[pb_] = R[i]["yT_b"].T
        ns[pb_] = so[4]
    return (y_p, y_s, ns)
```
